# Optimizing a Trainium2 kernel written in Bass

```python
import math
import jax, jax.numpy as jnp
from jax import lax
import numpy as np

D_MODEL = 1024
BATCH = 32
SEQ = 2048
DEPTH = 1

HEAD_DIM = 64
MIX_WIDTH = D_MODEL
ATTN_WIDTH = MIX_WIDTH // 2
FOURIER_WIDTH = MIX_WIDTH - ATTN_WIDTH
N_FOURIER_GROUPS = FOURIER_WIDTH // HEAD_DIM
N_Q_HEADS = ATTN_WIDTH // HEAD_DIM
Q_PER_KV = 4
N_KV_HEADS = N_Q_HEADS // Q_PER_KV
KV_WIDTH = N_KV_HEADS * HEAD_DIM
IN_PROJ_WIDTH = FOURIER_WIDTH + ATTN_WIDTH + 2 * KV_WIDTH
WINDOW = 128
BLOCK = 128
N_BUCKETS = 32
MAX_DISTANCE = 128
N_EXPERTS = 16
CAPACITY_FACTOR = 2
D_EXPERT = 1024
N_ADA = 6
EPS = 1e-6

kernel_name = "hybrid_fourier_window_gqa_ec_moe_block"


def rmsnorm(x, g):
    xf = x.astype(jnp.float32)
    y = xf * lax.rsqrt(jnp.mean(xf * xf, axis=-1, keepdims=True) + EPS)
    return (y * g.astype(jnp.float32)).astype(x.dtype)


def t5_bucket(rel):
    half = N_BUCKETS // 2
    max_exact = half // 2
    ret = jnp.where(rel > 0, half, 0)
    n = jnp.abs(rel)
    nf = jnp.maximum(n, 1).astype(jnp.float32)
    large = max_exact + (jnp.log(nf / max_exact) / math.log(MAX_DISTANCE / max_exact)
                         * (half - max_exact)).astype(jnp.int32)
    large = jnp.minimum(large, half - 1)
    return ret + jnp.where(n < max_exact, n, large)


def fourier_mix(u, w_f, b_f):
    B, S, _ = u.shape
    ug = u.reshape(B, S, N_FOURIER_GROUPS, HEAD_DIM).astype(jnp.float32)
    mixed = jnp.fft.fft2(ug, axes=(1, 3), norm="ortho").real.astype(u.dtype)
    y = jnp.einsum('bsgc,gcd->bsgd', mixed, w_f) + b_f
    return y.reshape(B, S, FOURIER_WIDTH)


def windowed_gqa(q, k, v, rel_bias, sink):
    B, S = q.shape[:2]
    nb = S // BLOCK
    span = BLOCK + 2 * WINDOW
    pad = ((0, 0), (WINDOW, WINDOW), (0, 0), (0, 0))
    kp = jnp.pad(k, pad)
    vp = jnp.pad(v, pad)
    rel = jnp.arange(span)[None, :] - WINDOW - jnp.arange(BLOCK)[:, None]
    bias = rel_bias[t5_bucket(rel)].astype(jnp.float32)
    bias = bias.transpose(2, 0, 1).reshape(N_KV_HEADS, Q_PER_KV, BLOCK, span)
    band = jnp.abs(rel) <= WINDOW
    sink_l = sink.astype(jnp.float32).reshape(N_KV_HEADS, Q_PER_KV, 1, 1)
    scale = HEAD_DIM ** -0.5

    def one_block(i):
        start = i * BLOCK
        qb = lax.dynamic_slice_in_dim(q, start, BLOCK, axis=1)
        kb = lax.dynamic_slice_in_dim(kp, start, span, axis=1)
        vb = lax.dynamic_slice_in_dim(vp, start, span, axis=1)
        kpos = start - WINDOW + jnp.arange(span)
        valid = band & ((kpos >= 0) & (kpos < S))[None, :]
        logits = jnp.einsum('bqkrd,bjkd->bkrqj', qb, kb).astype(jnp.float32) * scale + bias
        logits = jnp.where(valid, logits, -jnp.inf)
        m = jnp.maximum(jnp.max(logits, axis=-1, keepdims=True), sink_l)
        p = jnp.exp(logits - m)
        denom = jnp.sum(p, axis=-1, keepdims=True) + jnp.exp(sink_l - m)
        return jnp.einsum('bkrqj,bjkd->bqkrd', (p / denom).astype(vb.dtype), vb)

    outs = lax.map(one_block, jnp.arange(nb))
    return outs.transpose(1, 0, 2, 3, 4, 5).reshape(B, S, ATTN_WIDTH)


def expert_choice_moe(h, w_router, w_gate, w_up, w_down):
    B, S, D = h.shape
    cap = CAPACITY_FACTOR * S // N_EXPERTS
    aff = jax.nn.softmax(jnp.einsum('bsd,de->bse', h, w_router).astype(jnp.float32), axis=-1)
    g, idx = lax.top_k(aff.transpose(0, 2, 1), cap)
    xin = jax.vmap(lambda hb, ib: hb[ib])(h, idx)
    a = jnp.einsum('becd,edf->becf', xin, w_gate)
    u = jnp.einsum('becd,edf->becf', xin, w_up)
    y = jnp.einsum('becf,efd->becd', jax.nn.silu(a) * u, w_down)
    y = y * g[..., None].astype(y.dtype)
    return jax.vmap(lambda yb, ib: jnp.zeros((S, D), yb.dtype)
                    .at[ib.reshape(-1)].add(yb.reshape(-1, D)))(y, idx)


def setup_inputs(seed: int = 0) -> dict:
    key = jax.random.key(seed)
    ks = jax.random.split(key, 18)
    f32 = jnp.float32
    nrm = lambda k, shape, s: jax.random.normal(k, shape, f32) * s
    L, D = DEPTH, D_MODEL
    return {
        "x": nrm(ks[0], (BATCH, SEQ, D), 1.0),
        "c": nrm(ks[1], (BATCH, D), 1.0),
        "rel_bias": nrm(ks[2], (N_BUCKETS, N_Q_HEADS), 0.5),
        "w_ada": nrm(ks[3], (L, D, N_ADA * D), 0.5 * D ** -0.5),
        "b_ada": nrm(ks[4], (L, N_ADA * D), 0.02),
        "norm_mix_g": 1.0 + nrm(ks[5], (L, D), 0.02),
        "norm_ffn_g": 1.0 + nrm(ks[6], (L, D), 0.02),
        "w_in": nrm(ks[7], (L, D, IN_PROJ_WIDTH), D ** -0.5),
        "w_fourier": nrm(ks[8], (L, N_FOURIER_GROUPS, HEAD_DIM, HEAD_DIM), HEAD_DIM ** -0.5),
        "b_fourier": nrm(ks[9], (L, N_FOURIER_GROUPS, HEAD_DIM), 0.02),
        "q_norm_g": 1.0 + nrm(ks[10], (L, HEAD_DIM), 0.02),
        "k_norm_g": 1.0 + nrm(ks[11], (L, HEAD_DIM), 0.02),
        "sink": nrm(ks[12], (L, N_Q_HEADS), 1.0),
        "w_out": nrm(ks[13], (L, MIX_WIDTH, D), MIX_WIDTH ** -0.5),
        "w_router": nrm(ks[14], (L, D, N_EXPERTS), D ** -0.5),
        "w_gate": nrm(ks[15], (L, N_EXPERTS, D, D_EXPERT), D ** -0.5),
        "w_up": nrm(ks[16], (L, N_EXPERTS, D, D_EXPERT), D ** -0.5),
        "w_down": nrm(ks[17], (L, N_EXPERTS, D_EXPERT, D), D_EXPERT ** -0.5),
    }


def reference(x, c, rel_bias, w_ada, b_ada, norm_mix_g, norm_ffn_g, w_in, w_fourier,
              b_fourier, q_norm_g, k_norm_g, sink, w_out, w_router, w_gate, w_up, w_down):
    B, S, _ = x.shape
    split_cols = [FOURIER_WIDTH, FOURIER_WIDTH + ATTN_WIDTH, FOURIER_WIDTH + ATTN_WIDTH + KV_WIDTH]
    c_act = jax.nn.silu(c)
    for l in range(DEPTH):
        mod = jnp.einsum('bd,de->be', c_act, w_ada[l]) + b_ada[l]
        sh1, sc1, g1, sh2, sc2, g2 = [m[:, None, :] for m in jnp.split(mod, N_ADA, axis=-1)]

        h = rmsnorm(x, norm_mix_g[l]) * (1.0 + sc1) + sh1
        proj = jnp.einsum('bsd,de->bse', h, w_in[l])
        u_f, q, k, v = jnp.split(proj, split_cols, axis=-1)
        q = rmsnorm(q.reshape(B, S, N_KV_HEADS, Q_PER_KV, HEAD_DIM), q_norm_g[l])
        k = rmsnorm(k.reshape(B, S, N_KV_HEADS, HEAD_DIM), k_norm_g[l])
        v = v.reshape(B, S, N_KV_HEADS, HEAD_DIM)
        y_f = fourier_mix(u_f, w_fourier[l], b_fourier[l])
        y_a = windowed_gqa(q, k, v, rel_bias, sink[l])
        mixed = jnp.einsum('bse,ed->bsd', jnp.concatenate([y_f, y_a], axis=-1), w_out[l])
        x = x + g1 * mixed

        h2 = rmsnorm(x, norm_ffn_g[l]) * (1.0 + sc2) + sh2
        x = x + g2 * expert_choice_moe(h2, w_router[l], w_gate[l], w_up[l], w_down[l])
    return x
```

```python
from contextlib import ExitStack
import math
import numpy as np
import ml_dtypes
import concourse.bass as bass
import concourse.mybir as mybir
from concourse.bass_utils import run_bass_kernel_spmd

F32 = mybir.dt.float32
BF16 = mybir.dt.bfloat16
U32 = mybir.dt.uint32
ALU = mybir.AluOpType
AF = mybir.ActivationFunctionType
AX = mybir.AxisListType

NCORES = 8
NB = 4
S = 2048
D = 1024
NT = S // 128
NE = 16
CAP = 256
EPS = 1e-6
NWB = 5
NYV = 3


class Prog:
    ENG = ("pe", "act", "dve", "pool", "sp")

    def __init__(self, nc, stack):
        self.nc = nc
        self.stack = stack
        self.eng = {"pe": nc.tensor, "act": nc.scalar, "dve": nc.vector,
                    "pool": nc.gpsimd, "sp": nc.sync}
        self.ops = {k: [] for k in self.ENG}
        self.cnt = {k: 0 for k in self.ENG}
        self.esem = {k: stack.enter_context(nc.semaphore("es_" + k)) for k in self.ENG}
        self.waited = {k: {} for k in self.ENG}
        self.res_w = {}
        self.res_r = {}
        self.dsem = {}
        self.nops = 0

    def _dma_sem(self, key):
        if key not in self.dsem:
            s = self.stack.enter_context(self.nc.semaphore("ds_%d" % len(self.dsem)))
            self.dsem[key] = [s, 0]
        return self.dsem[key]

    class _Rec:
        def __init__(self):
            self.name, self.a, self.k = None, (), {}

        def __getattr__(self, name):
            def f(*a, **k):
                self.name, self.a, self.k = name, a, k
                return self
            return f

        def then_inc(self, *a, **k):
            return self

    def begin_sched(self):
        self.buf = []

    def _cost(self, engine, fn, dma):
        r = Prog._Rec()
        fn(r)
        def fsz(ap):
            n = 1
            for d in list(ap.shape)[1:]:
                n *= int(d)
            return n
        if dma is not None:
            ap = r.k.get("out", r.a[0] if r.a else None)
            nbytes = fsz(ap) * int(ap.shape[0]) * 4
            occ = 0.06 if engine != "pool" else 1.0
            return occ, 2.0 + nbytes / 120e3
        if engine == "pe":
            rhs = r.a[2] if len(r.a) > 2 else r.k.get("rhs")
            lhs = r.a[1] if len(r.a) > 1 else r.k.get("lhsT")
            n = fsz(rhs)
            mult = 4.0 if str(lhs.dtype).endswith("float32") else 1.0
            return mult * max(n, 64) / 2400.0 + 0.01, 0.06
        out = r.k.get("out", r.a[0] if r.a else None)
        n = fsz(out) if out is not None else 64
        if engine == "act":
            return 0.2 + n / 1200.0, 0.06
        if engine == "dve":
            return 0.07 + max(n, 64) / 960.0, 0.06
        return 0.3 + n / 500.0, 0.1

    def end_sched(self):
        buf, self.buf = self.buf, None
        n = len(buf)
        preds = [set() for _ in range(n)]
        last_w, readers = {}, {}
        for i, (engine, fn, reads, writes, dma) in enumerate(buf):
            for r in reads:
                if r in last_w:
                    preds[i].add(last_w[r])
                if isinstance(r, tuple) and r[0] == "ps":
                    for t in readers.get(r, ()):
                        if buf[t][0] != engine:
                            preds[i].add(t)
            for w in writes:
                if w in last_w:
                    preds[i].add(last_w[w])
                for t in readers.get(w, ()):
                    preds[i].add(t)
            for r in reads:
                readers.setdefault(r, []).append(i)
            for w in writes:
                last_w[w] = i
                readers[w] = []
            preds[i].discard(i)
        succs = [[] for _ in range(n)]
        npred = [len(p) for p in preds]
        for i, p in enumerate(preds):
            for j in p:
                succs[j].append(i)
        costs = [self._cost(b[0], b[1], b[4]) for b in buf]
        import heapq
        ready_t = [0.0] * n
        cand = {e: [] for e in self.ENG}
        for i in range(n):
            if npred[i] == 0:
                heapq.heappush(cand[buf[i][0]], (0.0, i))
        free_at = {e: 0.0 for e in self.ENG}
        order = []
        done = 0
        while done < n:
            best = None
            for e in self.ENG:
                h = cand[e]
                if not h:
                    continue
                bi = None
                for (rt, i) in h:
                    stt = max(rt, free_at[e])
                    key = (stt, i)
                    if bi is None or key < bi[0]:
                        bi = (key, rt, i)
                if best is None or bi[0] < best[0]:
                    best = (bi[0], bi[1], bi[2], e)
            (stt, _), rt, i, e = best
            cand[e].remove((rt, i))
            heapq.heapify(cand[e])
            occ, lat = costs[i]
            free_at[e] = stt + occ
            fin = stt + occ + lat
            order.append((stt, i))
            done += 1
            for j in succs[i]:
                ready_t[j] = max(ready_t[j], fin)
                npred[j] -= 1
                if npred[j] == 0:
                    heapq.heappush(cand[buf[j][0]], (ready_t[j], j))
        order.sort()
        self.sim_time = max(free_at.values())
        for _, i in order:
            engine, fn, reads, writes, dma = buf[i]
            self.op(engine, fn, reads, writes, dma)

    def op(self, engine, fn, reads=(), writes=(), dma=None):
        if getattr(self, "buf", None) is not None:
            self.buf.append((engine, fn, tuple(reads), tuple(writes), dma))
            return None
        deps = {}

        def need(tok, is_war=False):
            sem, val, e, d = tok
            if e == engine and dma is None and d is None:
                if engine == "pe" or is_war:
                    return
            k = id(sem)
            if k not in deps or deps[k][1] < val:
                deps[k] = (sem, val)

        for r in reads:
            if r in self.res_w:
                need(self.res_w[r])
            if isinstance(r, tuple) and r[0] == "ps":
                for t in self.res_r.get(r, ()):
                    if t[2] != engine:
                        need(t)
        for w in writes:
            if w in self.res_w:
                need(self.res_w[w])
            for t in self.res_r.get(w, ()):
                need(t, True)
        waits = []
        wd = self.waited[engine]
        for k, (sem, val) in deps.items():
            if wd.get(k, 0) >= val:
                continue
            wd[k] = val
            waits.append((sem, val))
        if dma is None:
            self.cnt[engine] += 1
            tok = (self.esem[engine], self.cnt[engine], engine, None)
            inc = 1
        else:
            d = self._dma_sem(dma)
            d[1] += 16
            tok = (d[0], d[1], engine, dma)
            inc = 16
        for r in reads:
            self.res_r.setdefault(r, []).append(tok)
        for w in writes:
            self.res_w[w] = tok
            self.res_r[w] = []
        eng = self.eng[engine]
        tsem = tok[0]

        def run():
            for sem, val in waits:
                eng.wait_ge(sem, val)
            fn(eng).then_inc(tsem, inc)

        self.ops[engine].append(run)
        self.nops += 1
        return tok

    def end_block(self):
        sp = self.eng["sp"]
        toks = [(d[0], d[1]) for d in self.dsem.values() if d[1] > 0]
        wd = self.waited["sp"]

        def run():
            for sem, val in toks:
                if wd.get(id(sem), 0) < val:
                    wd[id(sem)] = val
                    sp.wait_ge(sem, val)

        self.ops["sp"].append(run)
        ops = self.ops
        with self.nc.Block() as block:
            @block.tensor
            def _(e):
                for f in ops["pe"]:
                    f()

            @block.scalar
            def _(e):
                for f in ops["act"]:
                    f()

            @block.vector
            def _(e):
                for f in ops["dve"]:
                    f()

            @block.gpsimd
            def _(e):
                for f in ops["pool"]:
                    f()

            @block.sync
            def _(e):
                for f in ops["sp"]:
                    f()
        self.ops = {k: [] for k in self.ENG}


def _bucket_table():
    import jax
    import jax.numpy as jnp
    with jax.default_device(jax.devices("cpu")[0]):
        rel = jnp.arange(-255, 256, dtype=jnp.int32)
        half = 16
        max_exact = 8
        ret = jnp.where(rel > 0, half, 0)
        n = jnp.abs(rel)
        nf = jnp.maximum(n, 1).astype(jnp.float32)
        large = max_exact + (jnp.log(nf / max_exact) / math.log(128 / max_exact)
                             * (half - max_exact)).astype(jnp.int32)
        large = jnp.minimum(large, half - 1)
        return np.asarray(ret + jnp.where(n < max_exact, n, large)).astype(np.int64)


def _consts():
    c = {}
    c["ident"] = np.eye(128, dtype=np.float32).astype(ml_dtypes.bfloat16)
    c["jmat"] = np.eye(128, dtype=np.float32)[::-1].copy().astype(ml_dtypes.bfloat16)
    c["identf"] = np.eye(128, dtype=np.float32)
    t = np.arange(S, dtype=np.int64)
    ang = 2.0 * np.pi * ((t[:, None] * t[None, :]) % S).astype(np.float64) / S
    def lay(m):
        return np.ascontiguousarray(m.reshape(NT, 128, S // 256, 256).transpose(2, 1, 0, 3))
    c["cmat"] = lay(np.cos(ang).astype(np.float32).astype(ml_dtypes.bfloat16))
    c["smat"] = lay((-np.sin(ang)).astype(np.float32).astype(ml_dtypes.bfloat16))
    k = np.arange(64, dtype=np.int64)
    a64 = 2.0 * np.pi * ((k[:, None] * k[None, :]) % 64).astype(np.float64) / 64
    cc = np.cos(a64).astype(np.float32)
    sc = np.sin(a64).astype(np.float32)
    c["ccdup"] = np.concatenate([cc, cc], axis=1)
    c["scdup"] = np.concatenate([sc, sc], axis=1)
    bk = _bucket_table()
    ohr = np.zeros((32, 511), np.float32)
    maskr = np.zeros((8, 511), np.float32)
    for m in range(511):
        r = 255 - m
        if abs(r) <= 128:
            ohr[bk[r + 255], m] = 1.0
            maskr[:, m] = 1.0
    c["ohr"] = ohr
    c["maskr"] = maskr
    sel = np.zeros((4, 4, 128), np.float32)
    for b in range(4):
        sel[b, b, :] = 1.0
    c["sel4"] = sel.transpose(1, 0, 2).copy()
    bd = np.zeros((64, 64), np.float32)
    for b in range(4):
        bd[16 * b:16 * b + 16, 16 * b:16 * b + 16] = 1.0
    c["bd"] = bd
    return c


CONST_SPECS = [("ident", [128, 128], BF16), ("jmat", [128, 128], BF16), ("identf", [128, 128], F32),
               ("cmat", [S // 256, 128, NT, 256], BF16), ("smat", [S // 256, 128, NT, 256], BF16), ("ccdup", [64, 128], F32),
               ("scdup", [64, 128], F32), ("ohr", [32, 511], F32), ("maskr", [8, 511], F32),
               ("sel4", [4, 4, 128], F32), ("bd", [64, 64], F32)]

IN_SPECS = [("x", [NB, S, D]), ("cT", [D, NB]), ("rel_bias", [32, 8]), ("w_ada", [D, 6 * D]),
            ("b_adaT", [128, 48]), ("b_ada_g", [2, D]), ("gmixT", [128, 8]), ("gffnT", [128, 8]),
            ("w_qkv", [D, 768]), ("w_in_fT", [512, D]), ("w_fourier", [8, 64, 64]), ("bfT", [128, 4]),
            ("gqk_row", [640]), ("sink", [8]), ("w_out", [D, D]), ("w_router", [D, NE]),
            ("w_gate", [NE, D, D]), ("w_up", [NE, D, D]), ("w_down", [NE, D, D])]


def build(mode="full"):
    nc = bass.Bass("TRN2", target_bir_lowering=False)
    I = {}
    for name, shape in IN_SPECS:
        I[name] = nc.dram_tensor(name, shape, F32, kind="ExternalInput").ap()
    for name, shape, dt in CONST_SPECS:
        I[name] = nc.dram_tensor(name, shape, dt, kind="ExternalInput").ap()
    out = nc.dram_tensor("out", [NB, S, D], F32, kind="ExternalOutput").ap()
    h2s = nc.dram_tensor("h2s", [NB, S, D], BF16, kind="Internal").ap()
    wv_t = nc.dram_tensor("wv", [8, 511], BF16, kind="Internal")
    wv = wv_t.ap()
    x = I["x"]
    h2s_flat = h2s.rearrange("b s d -> (b s) d")
    out_flat = out.rearrange("b s d -> (b s) d")
    if mode == "mixer":
        dbg_yf = nc.dram_tensor("dbg_yf", [128, 4, S], BF16, kind="ExternalOutput").ap()
        dbg_ya = nc.dram_tensor("dbg_ya", [128, 4, S], BF16, kind="ExternalOutput").ap()

    with ExitStack() as st:
        P = Prog(nc, st)

        uid = [0]

        def sbt(stack, name, shape, dt):
            uid[0] += 1
            return stack.enter_context(nc.sbuf_tensor("sb%d_%s" % (uid[0], name), shape, dt))

        ps = [st.enter_context(nc.psum_tensor("ps%d" % i, [128, 512], F32)) for i in range(8)]
        PSK = [("ps", i) for i in range(8)]

        def dve(fn, r=(), w=()):
            return P.op("dve", fn, r, w)

        def act(fn, r=(), w=()):
            return P.op("act", fn, r, w)

        def pe(fn, r=(), w=()):
            return P.op("pe", fn, r, w)

        def pool(fn, r=(), w=()):
            return P.op("pool", fn, r, w)

        def dma(q, out_, in_, r=(), w=(), key=None, **kw):
            return P.op(q, lambda e: e.dma_start(out=out_, in_=in_, **kw), r, w, dma=key)

        def mm(o, l, rh, start, stop, r, w):
            return pe(lambda e: e.matmul(o, l, rh, start=start, stop=stop), r, w)

        stA = ExitStack()
        ident = sbt(st, "ident", [128, 128], BF16)
        jmat = sbt(st, "jmat", [128, 128], BF16)
        identf = sbt(st, "identf", [128, 128], F32)
        epsc = sbt(st, "epsc", [128, 1], F32)
        colG1 = sbt(st, "colG1", [128, NB, 8], F32)
        colSH1 = sbt(st, "colSH1", [128, NB, 8], F32)
        colG2 = sbt(st, "colG2", [128, NB, 8], F32)
        colSH2 = sbt(st, "colSH2", [128, NB, 8], F32)
        modrow = sbt(st, "modrow", [4, 2, D], F32)
        sel4 = sbt(st, "sel4", [4, 4, 128], F32)
        bd = sbt(st, "bd", [64, 64], F32)
        idxU = sbt(st, "idxU", [128, 2, 64], U32)
        valT = sbt(st, "valT", [128, 2, 64], F32)
        g2bc = sbt(st, "g2bc", [128, NB, D], F32)
        stL = ExitStack()
        logit = sbt(stL, "logit", [64, S], F32)
        W_A = sbt(stA, "W_A", [128, 8, 512], BF16)
        W_B = sbt(stA, "W_B", [128, 8, 512], BF16)
        expbT = sbt(stA, "expbT", [128, 8, 3, 128], BF16)
        gqk = sbt(stA, "gqk", [128, 640], F32)
        expsink = sbt(stA, "expsink", [128, 8], F32)
        bfT = sbt(stA, "bfT", [128, 4], F32)
        w_r4 = sbt(stA, "w_r4", [128, 8, 4, 64], BF16)

        dma("sp", ident[:], I["ident"], w=["ident"], key="c0")
        dma("sp", jmat[:], I["jmat"], w=["jmat"], key="c1")
        dma("sp", identf[:], I["identf"], w=["identf"], key="c2")
        dma("sp", sel4[:], I["sel4"], w=["sel4"], key="c3")
        dma("sp", bfT[:], I["bfT"], w=["bfT"], key="c4")
        dma("sp", bd[:], I["bd"], w=["bd"], key="c5")
        dma("sp", gqk[:], I["gqk_row"].partition_broadcast(128), w=["gqk"], key="c6")
        dve(lambda e: e.memset(epsc[:], EPS), w=["epsc"])

        with ExitStack() as s0:
            cT = sbt(s0, "cT", [128, 8, NB], F32)
            siluT = sbt(s0, "siluT", [128, 8, NB], F32)
            stmp = sbt(s0, "stmp", [128, 8, NB], F32)
            b_adaT = sbt(s0, "b_adaT", [128, 48], F32)
            gmixT = sbt(s0, "gmixT", [128, 8], F32)
            gffnT = sbt(s0, "gffnT", [128, 8], F32)
            modT = sbt(s0, "modT", [128, 48, NB], F32)
            badar = sbt(s0, "badar", [4, 2, D], F32)
            wa = [sbt(s0, "wa%d" % i, [128, 8, 512], BF16) for i in range(4)]
            siluTb = sbt(s0, "siluTb", [128, 8, NB], BF16)
            modrow_all = sbt(s0, "modrow_all", [4, 6 * D], F32)
            sinkb = sbt(s0, "sinkb", [128, 8], F32)
            rb = sbt(s0, "rb", [32, 8], F32)
            ohr = sbt(s0, "ohr", [32, 511], F32)
            maskr = sbt(s0, "maskr", [8, 511], F32)
            ev = sbt(s0, "ev", [8, 511], F32)
            wvs = sbt(s0, "wvs", [8, 511], BF16)
            wf = sbt(s0, "wf", [64, 8, 64], F32)
            ccdup = sbt(s0, "ccdup", [64, 128], F32)
            scdup = sbt(s0, "scdup", [64, 128], F32)
            winfT = sbt(s0, "winfT", [128, 4, D], F32)
            bdA = sbt(s0, "bdA", [128, 4, 128], F32)
            bdB = sbt(s0, "bdB", [128, 4, 128], F32)
            wr_f = sbt(s0, "wr_f", [128, 8, NE], F32)

            dma("sp", cT[:], I["cT"].rearrange("(k p) b -> p k b", p=128), w=["cT"], key="s0")
            dma("sp", b_adaT[:], I["b_adaT"], w=["b_adaT"], key="s1")
            dma("sp", gmixT[:], I["gmixT"], w=["gmixT"], key="s2")
            dma("sp", gffnT[:], I["gffnT"], w=["gffnT"], key="s3")
            for j in range(2):
                dma("sp", badar[:, j, :], I["b_ada_g"][j].partition_broadcast(4), w=["badar%d" % j], key="s4%d" % j)
            dma("sp", sinkb[:], I["sink"].partition_broadcast(128), w=["sinkb"], key="s5")
            dma("sp", rb[:], I["rel_bias"], w=["rb"], key="s6")
            dma("sp", ohr[:], I["ohr"], w=["ohr"], key="s7")
            dma("sp", maskr[:], I["maskr"], w=["maskr"], key="s8")
            dma("sp", wf[:], I["w_fourier"].rearrange("g k d -> k g d"), w=["wf"], key="s9")
            dma("sp", ccdup[:], I["ccdup"], w=["ccdup"], key="s10")
            dma("sp", scdup[:], I["scdup"], w=["scdup"], key="s11")
            dma("sp", winfT[:], I["w_in_fT"].rearrange("(j p) m -> p j m", p=128), w=["winfT"], key="s12")
            dma("sp", wr_f[:], I["w_router"].rearrange("(k p) e -> p k e", p=128), w=["wr_f"], key="s13")

            act(lambda e: e.activation(out=stmp[:], in_=cT[:], func=AF.Exp, scale=-1.0), ["cT"], ["stmp"])
            dve(lambda e: e.tensor_scalar(out=stmp[:], in0=stmp[:], scalar1=1.0, scalar2=None, op0=ALU.add), ["stmp"], ["stmp"])
            dve(lambda e: e.reciprocal(out=stmp[:], in_=stmp[:]), ["stmp"], ["stmp"])
            dve(lambda e: e.tensor_tensor(out=siluT[:], in0=cT[:], in1=stmp[:], op=ALU.mult), ["cT", "stmp"], ["siluT"])

            dve(lambda e: e.tensor_copy(out=siluTb[:], in_=siluT[:]), ["siluT"], ["siluTb"])
            for pc in range(12):
                a, hh = pc // 2, pc % 2
                wsl = wa[pc % 4]
                wk = "wa%d" % (pc % 4)
                P.op("pool", lambda e, wsl=wsl, pc=pc: e.dma_start(
                    out=wsl[:], in_=I["w_ada"][:, pc * 512:(pc + 1) * 512].rearrange("(k p) e -> p k e", p=128)),
                    (), [wk], dma=wk)
                pr = pc % 2
                for k in range(8):
                    mm(ps[pr][0:4, :], siluTb[:, k, :], wsl[:, k, :], k == 0, k == 7, [wk, "siluTb"], [PSK[pr]])
                dve(lambda e, pc=pc, pr=pr: e.tensor_copy(out=modrow_all[:, pc * 512:(pc + 1) * 512], in_=ps[pr][0:4, :]),
                    [PSK[pr]], [("mra", pc)])
                if a in (2, 5):
                    gi = 0 if a == 2 else 1
                    dve(lambda e, gi=gi, hh=hh, pc=pc: e.tensor_tensor(out=modrow[:, gi, hh * 512:(hh + 1) * 512],
                                                                       in0=modrow_all[:, pc * 512:(pc + 1) * 512],
                                                                       in1=badar[:, gi, hh * 512:(hh + 1) * 512], op=ALU.add),
                        [("mra", pc), "badar%d" % gi], ["modrow"])
            psM = ps[2]
            for j in range(48):
                mm(psM[:, j * 4:j * 4 + 4], modrow_all[:, j * 128:(j + 1) * 128], identf[0:4, 0:4], True, True,
                   [("mra", j // 4), "identf"], [PSK[2]])
            dve(lambda e: e.tensor_tensor(out=modT[:], in0=psM[:, 0:192].rearrange("p (j b) -> p j b", b=NB),
                                          in1=b_adaT[:].unsqueeze(2).to_broadcast([128, 48, NB]), op=ALU.add),
                [PSK[2], "b_adaT"], ["modT"])
            for b in range(NB):
                dve(lambda e, b=b: e.tensor_copy(out=colSH1[:, b, :], in_=modT[:, 0:8, b]), ["modT"], ["colSH1"])
                dve(lambda e, b=b: e.scalar_tensor_tensor(out=colG1[:, b, :], in0=modT[:, 8:16, b], scalar=1.0, in1=gmixT[:],
                                                          op0=ALU.add, op1=ALU.mult), ["modT", "gmixT"], ["colG1"])
                dve(lambda e, b=b: e.tensor_copy(out=colSH2[:, b, :], in_=modT[:, 24:32, b]), ["modT"], ["colSH2"])
                dve(lambda e, b=b: e.scalar_tensor_tensor(out=colG2[:, b, :], in0=modT[:, 32:40, b], scalar=1.0, in1=gffnT[:],
                                                          op0=ALU.add, op1=ALU.mult), ["modT", "gffnT"], ["colG2"])

            mm(ps[2][0:8, 0:511], rb[:], ohr[:], True, True, ["rb", "ohr"], [PSK[2]])
            act(lambda e: e.activation(out=ev[:], in_=ps[2][0:8, 0:511], func=AF.Exp), [PSK[2]], ["ev"])
            dve(lambda e: e.tensor_tensor(out=wvs[:], in0=ev[:], in1=maskr[:], op=ALU.mult), ["ev", "maskr"], ["wvs"])
            dma("sp", wv, wvs[:], r=["wvs"], w=["wv"], key="wv")
            for kt in range(3):
                src = bass.AP(tensor=wv_t, offset=256 - 128 * kt, ap=[[1, 128], [511, 8], [1, 128]])
                dma("sp", expbT[:, :, kt, :], src, r=["wv"], w=["expbT%d" % kt], key="eb%d" % kt)
            act(lambda e: e.activation(out=expsink[:], in_=sinkb[:], func=AF.Exp), ["sinkb"], ["expsink"])

            fscale = 1.0 / math.sqrt(S * 64.0)
            for (cd, cdk, bdX, bdk, psi) in ((ccdup, "ccdup", bdA, "bdA", 3), (scdup, "scdup", bdB, "bdB", 4)):
                for g in range(8):
                    mm(ps[psi][:, g * 64:(g + 1) * 64], cd[:], wf[:, g, :], True, True, [cdk, "wf"], [PSK[psi]])
                pool(lambda e, bdX=bdX: e.memset(bdX[:], 0.0), w=[bdk])
                for j in range(4):
                    dve(lambda e, bdX=bdX, psi=psi, j=j: e.tensor_scalar(
                        out=bdX[0:64, j, 0:64], in0=ps[psi][0:64, (2 * j) * 64:(2 * j + 1) * 64],
                        scalar1=fscale, scalar2=None, op0=ALU.mult), [PSK[psi], bdk], [bdk])
                    dve(lambda e, bdX=bdX, psi=psi, j=j: e.tensor_scalar(
                        out=bdX[64:128, j, 64:128], in0=ps[psi][64:128, (2 * j + 1) * 64:(2 * j + 2) * 64],
                        scalar1=fscale, scalar2=None, op0=ALU.mult), [PSK[psi], bdk], [bdk])
            for (bdX, bdk, WX, wk2, pbase) in ((bdA, "bdA", W_A, "W_A", 5), (bdB, "bdB", W_B, "W_B", 6)):
                for m in range(8):
                    for j in range(4):
                        mm(ps[pbase][:, j * 128:(j + 1) * 128], winfT[:, j, m * 128:(m + 1) * 128], bdX[:, j, :],
                           True, True, ["winfT", bdk], [PSK[pbase]])
                    act(lambda e, WX=WX, m=m, pbase=pbase: e.activation(out=WX[:, m, :], in_=ps[pbase][:, :], func=AF.Copy),
                        [PSK[pbase]], [wk2])

            pool(lambda e: e.memset(w_r4[:], 0.0), w=["w_r4"])
            for b in range(NB):
                dve(lambda e, b=b: e.tensor_copy(out=w_r4[:, :, b, 16 * b:16 * b + 16], in_=wr_f[:]), ["wr_f", "w_r4"], ["w_r4"])
            P.end_block()

        for b in range(NB):
            with ExitStack() as sa:
                A_all = sbt(sa, "A_all", [128, NT, 512], BF16)
                B_all = sbt(sa, "B_all", [128, NT, 512], BF16)
                yfT = sbt(sa, "yfT", [128, 4, S], BF16)
                yaT = sbt(sa, "yaT", [128, 4, S], BF16)
                with ExitStack() as s1:
                    NXT, NQ, NK, NV = 3, 4, 5, 8
                    w_qkv = sbt(s1, "w_qkv", [128, 8, 768], BF16)
                    xt = [sbt(s1, "xt%d" % i, [128, D], F32) for i in range(NXT)]
                    xn = [sbt(s1, "xn%d" % i, [128, D], BF16) for i in range(2)]
                    hT = [sbt(s1, "hT%d" % i, [128, 8, 128], BF16) for i in range(2)]
                    st4 = [sbt(s1, "st4_%d" % i, [128, 4], F32) for i in range(2)]
                    qkf = [sbt(s1, "qkf%d" % i, [128, 640], F32) for i in range(2)]
                    sq = [sbt(s1, "sq%d" % i, [128, 640], F32) for i in range(2)]
                    ssq = [sbt(s1, "ssq%d" % i, [128, 10], F32) for i in range(2)]
                    rq = [sbt(s1, "rq%d" % i, [128, 10], F32) for i in range(2)]
                    qkn = [sbt(s1, "qkn%d" % i, [128, 640], BF16) for i in range(2)]
                    vtmp = [sbt(s1, "vtmp%d" % i, [128, 128], BF16) for i in range(2)]
                    qT = [sbt(s1, "qT%d" % i, [64, 8 * 128], BF16) for i in range(NQ)]
                    kT = [sbt(s1, "kT%d" % i, [64, 2 * 128], BF16) for i in range(NK)]
                    Vr = [sbt(s1, "Vr%d" % i, [128, 2 * 65], BF16) for i in range(NV)]
                    Eb = [sbt(s1, "Eb%d" % i, [128, 512], BF16) for i in range(6)]
                    PT = [sbt(s1, "PT%d" % i, [128, 512], BF16) for i in range(12)]
                    den = [sbt(s1, "den%d" % i, [128, 4], F32) for i in range(4)]
                    ya = [sbt(s1, "ya%d" % i, [128, 512], BF16) for i in range(2)]
                    psb = [p_[:].bitcast(BF16) for p_ in ps]

                    P.begin_sched()
                    P.op("pool", lambda e: e.dma_start(out=w_qkv[:], in_=I["w_qkv"].rearrange("(k p) c -> p k c", p=128)),
                         (), ["w_qkv"], dma="wqkv")
                    for i in range(NV):
                        pool(lambda e, i=i: e.memset(Vr[i][:], 1.0), w=[("Vr", i)])
                    ecnt = [0]

                    def S0(i):
                        sl = i % NXT
                        dma("sp", xt[sl][:], x[b, i * 128:(i + 1) * 128, :], w=[("xt", sl)], key=("xt", sl))

                    def S1(i):
                        sl, s2 = i % NXT, i % 2
                        xk, nk, sk = ("xt", sl), ("xn", s2), ("st4", s2)
                        s4 = st4[s2]
                        act(lambda e: e.activation(out=xn[s2][:], in_=xt[sl][:], func=AF.Square, accum_out=s4[:, 0:1]),
                            [xk], [nk, sk])
                        act(lambda e: e.activation(out=s4[:, 1:2], in_=s4[:, 0:1], func=AF.Ln, bias=epsc[:], scale=1.0 / D),
                            [sk, "epsc"], [sk])
                        act(lambda e: e.activation(out=s4[:, 2:3], in_=s4[:, 1:2], func=AF.Exp, scale=-0.5), [sk], [sk])
                        act(lambda e: e.activation(out=xn[s2][:], in_=xt[sl][:], func=AF.Identity, scale=s4[:, 2:3]),
                            [xk, sk], [nk])

                    def S2(i):
                        s2 = i % 2
                        nk = ("xn", s2)
                        for k in range(8):
                            mm(ps[k // 4][:, (k % 4) * 128:(k % 4 + 1) * 128], xn[s2][:, k * 128:(k + 1) * 128], ident[:],
                               True, True, [nk, "ident"], [PSK[k // 4]])
                        for k in range(8):
                            if k < 8:
                                dve(lambda e, k=k: e.tensor_scalar(
                                    out=hT[s2][:, k, :], in0=ps[k // 4][:, (k % 4) * 128:(k % 4 + 1) * 128],
                                    scalar1=colG1[:, b, k:k + 1], scalar2=colSH1[:, b, k:k + 1], op0=ALU.mult, op1=ALU.add),
                                    [PSK[k // 4], "colG1", "colSH1"], [("hT", s2, 0)])
                            else:
                                act(lambda e, k=k: e.activation(
                                    out=hT[s2][:, k, :], in_=ps[1][:, (k % 4) * 128:(k % 4 + 1) * 128], func=AF.Identity,
                                    bias=colSH1[:, b, k:k + 1], scale=colG1[:, b, k:k + 1]),
                                    [PSK[1], "colG1", "colSH1"], [("hT", s2, 1)])

                    def S3(i):
                        s2 = i % 2
                        hk = [("hT", s2, 0), ("hT", s2, 1)]
                        hTs = hT[s2]
                        for k in range(8):
                            mm(ps[2][:, :], hTs[:, k, :], W_A[:, k, :], k == 0, k == 7, hk + ["W_A"], [PSK[2]])
                        act(lambda e: e.activation(out=A_all[:, i, :], in_=ps[2][:, :], func=AF.Copy), [PSK[2]], ["W_Aall"])
                        for k in range(8):
                            mm(ps[3][:, :], hTs[:, k, :], W_B[:, k, :], k == 0, k == 7, hk + ["W_B"], [PSK[3]])
                        dve(lambda e: e.tensor_copy(out=B_all[:, i, :], in_=ps[3][:, :]), [PSK[3]], ["W_Ball"])
                        for k in range(8):
                            mm(ps[2][:, :], hTs[:, k, :], w_qkv[:, k, 0:512], k == 0, k == 7, hk + ["w_qkv"], [PSK[2]])
                        for k in range(8):
                            mm(ps[3][:, 0:256], hTs[:, k, :], w_qkv[:, k, 512:768], k == 0, k == 7, hk + ["w_qkv"], [PSK[3]])
                        qf, vt = qkf[s2], vtmp[s2]
                        kq, kvt = ("qkf", s2), ("vtmp", s2)
                        act(lambda e: e.activation(out=qf[:, 0:512], in_=ps[2][:, :], func=AF.Copy), [PSK[2]], [kq])
                        dve(lambda e: e.tensor_copy(out=vt[:], in_=ps[3][:, 128:256]), [PSK[3]], [kvt])
                        dve(lambda e: e.tensor_copy(out=qf[:, 512:640], in_=ps[3][:, 0:128]), [PSK[3], kq], [kq])
                        mm(ps[3][:, 256:384], jmat[:], vt[:], True, True, ["jmat", kvt], [PSK[3]])
                        vk = ("Vr", i % NV)
                        dve(lambda e: e.tensor_copy(out=Vr[i % NV][:].rearrange("p (g c) -> p g c", g=2)[:, :, 0:64],
                                                    in_=ps[3][:, 256:384].rearrange("p (g c) -> p g c", g=2)),
                            [PSK[3]], [vk])

                    def S4(i):
                        s2 = i % 2
                        qf, sqq, ss_, rq_, qn = qkf[s2], sq[s2], ssq[s2], rq[s2], qkn[s2]
                        kq, ks, kss, krq, kqn = ("qkf", s2), ("sq", s2), ("ssq", s2), ("rq", s2), ("qkn", s2)
                        pool(lambda e: e.tensor_tensor(out=sqq[:], in0=qf[:], in1=qf[:], op=ALU.mult), [kq], [ks])
                        dve(lambda e: e.tensor_reduce(out=ss_[:], in_=sqq[:].rearrange("p (h c) -> p h c", h=10), axis=AX.X, op=ALU.add),
                            [ks], [kss])
                        act(lambda e: e.activation(out=rq_[:], in_=ss_[:], func=AF.Ln, bias=epsc[:], scale=1.0 / 64), [kss, "epsc"], [krq])
                        act(lambda e: e.activation(out=rq_[:], in_=rq_[:], func=AF.Exp, scale=-0.5), [krq], [krq])
                        pool(lambda e: e.tensor_tensor(out=sqq[:].rearrange("p (h c) -> p h c", h=10), in0=qf[:].rearrange("p (h c) -> p h c", h=10),
                                                       in1=rq_[:].unsqueeze(2).to_broadcast([128, 10, 64]), op=ALU.mult),
                             [kq, krq, ks], [ks])
                        pool(lambda e: e.tensor_tensor(out=qn[:], in0=sqq[:], in1=gqk[:], op=ALU.mult), [ks, "gqk"], [kqn])

                    def S5(i):
                        s2 = i % 2
                        qn, kqn = qkn[s2], ("qkn", s2)
                        qk_ = ("qT", i % NQ)
                        for h in range(8):
                            pb = 4 + (h // 4)
                            mm(ps[pb][0:64, (h % 4) * 128:(h % 4 + 1) * 128], qn[:, h * 64:(h + 1) * 64], ident[:],
                               True, True, [kqn, "ident"], [PSK[pb]])
                        for hb in range(2):
                            act(lambda e, hb=hb: e.activation(out=qT[i % NQ][:, hb * 512:(hb + 1) * 512], in_=ps[4 + hb][0:64, :], func=AF.Copy),
                                [PSK[4 + hb]], [qk_])
                        for g in range(2):
                            mm(ps[7][0:64, g * 128:(g + 1) * 128], qn[:, 512 + g * 64:512 + (g + 1) * 64], jmat[:],
                               True, True, [kqn, "jmat"], [PSK[7]])
                        act(lambda e: e.activation(out=kT[i % NK][:], in_=ps[7][0:64, 0:256], func=AF.Copy), [PSK[7]], [("kT", i % NK)])

                    def S6(ib):
                        qs = ib % NQ
                        kts = [kt for kt in (ib - 1, ib, ib + 1) if 0 <= kt < NT]
                        for g in range(2):
                            for kt in kts:
                                kidx = kt - ib + 1
                                ei = ecnt[0] % 6
                                ecnt[0] += 1
                                pt = (ib % 2) * 6 + g * 3 + kidx
                                psS = ps[4 + (ei % 2)]
                                psk = PSK[4 + (ei % 2)]
                                mm(psS[:, :], kT[kt % NK][:, g * 128:(g + 1) * 128], qT[qs][:, g * 512:(g + 1) * 512],
                                   True, True, [("kT", kt % NK), ("qT", qs)], [psk])
                                act(lambda e, psS=psS, ei=ei: e.activation(out=Eb[ei][:], in_=psS[:, :], func=AF.Exp, scale=0.125),
                                    [psk], [("Eb", ei)])
                                dve(lambda e, kidx=kidx, g=g, ei=ei, pt=pt: e.tensor_tensor(
                                    out=PT[pt][:].rearrange("p (h q) -> p h q", h=4), in0=Eb[ei][:].rearrange("p (h q) -> p h q", h=4),
                                    in1=expbT[:, 4 * g:4 * g + 4, kidx, :], op=ALU.mult),
                                    [("Eb", ei), "expbT%d" % kidx], [("PT", pt)])

                    def S7(ib):
                        kts = [kt for kt in (ib - 1, ib, ib + 1) if 0 <= kt < NT]
                        yk = ("ya", ib % 2)
                        for g in range(2):
                            psO = ps[6]
                            for hh in range(4):
                                for n, kt in enumerate(kts):
                                    kidx = kt - ib + 1
                                    pt = (ib % 2) * 6 + g * 3 + kidx
                                    mm(psO[:, hh * 65:(hh + 1) * 65], PT[pt][:, hh * 128:(hh + 1) * 128],
                                       Vr[kt % NV][:, g * 65:(g + 1) * 65], n == 0, n == len(kts) - 1,
                                       [("PT", pt), ("Vr", kt % NV)], [PSK[6]])
                            o3 = psO[:, 0:260].rearrange("p (h c) -> p h c", h=4)
                            di = (ib % 2) * 2 + g
                            dn = den[di]
                            dk = ("den", di)
                            dve(lambda e, o3=o3, g=g, dn=dn: e.tensor_tensor(out=dn[:], in0=o3[:, :, 64], in1=expsink[:, 4 * g:4 * g + 4], op=ALU.add),
                                [PSK[6], "expsink"], [dk])
                            dve(lambda e, dn=dn: e.reciprocal(out=dn[:], in_=dn[:]), [dk], [dk])
                            dve(lambda e, o3=o3, g=g, dn=dn: e.tensor_tensor(
                                out=ya[ib % 2][:, g * 256:(g + 1) * 256].rearrange("p (h c) -> p h c", h=4), in0=o3[:, :, 0:64],
                                in1=dn[:].unsqueeze(2).to_broadcast([128, 4, 64]), op=ALU.mult),
                                [PSK[6], dk], [yk])

                    def S8(ib):
                        yk = ("ya", ib % 2)
                        for cc in range(4):
                            mm(ps[7][:, cc * 128:(cc + 1) * 128], ya[ib % 2][:, cc * 128:(cc + 1) * 128], ident[:],
                               True, True, [yk, "ident"], [PSK[7]])
                        act(lambda e: e.activation(out=yaT[:, :, ib * 128:(ib + 1) * 128],
                                                   in_=ps[7][:, :].rearrange("p (c t) -> p c t", c=4), func=AF.Copy),
                            [PSK[7]], ["yaT"])

                    ok = lambda t: 0 <= t < NT
                    S0(0)
                    for j in range(NT + 9):
                        if ok(j + 1):
                            S0(j + 1)
                        if ok(j):
                            S1(j)
                        if ok(j - 6):
                            S6(j - 6)
                        if ok(j - 1):
                            S2(j - 1)
                        if ok(j - 7):
                            S7(j - 7)
                        if ok(j - 2):
                            S3(j - 2)
                        if ok(j - 3):
                            S4(j - 3)
                        if ok(j - 4):
                            S5(j - 4)
                        if ok(j - 8):
                            S8(j - 8)
                    P.end_sched()
                    P.end_block()
                s2 = ExitStack()
                if True:
                    SB = 256
                    Cs = [sbt(s2, "Cs%d" % i, [128, NT, SB], BF16) for i in range(2)]
                    Ss = [sbt(s2, "Ss%d" % i, [128, NT, SB], BF16) for i in range(2)]
                    P.begin_sched()

                    def dft(sbk):
                        sl = sbk % 2
                        s0_ = sbk * SB
                        dma("sp", Cs[sl][:], I["cmat"][sbk], w=[("Cs", sl)], key=("Cs", sl))
                        dma("sp", Ss[sl][:], I["smat"][sbk], w=[("Ss", sl)], key=("Ss", sl))
                        for cc in range(4):
                            pi = 6 + ((sbk * 4 + cc) % 2)
                            for tt in range(NT):
                                mm(ps[pi][:, 0:SB], A_all[:, tt, cc * 128:(cc + 1) * 128], Cs[sl][:, tt, :], tt == 0, False,
                                   ["W_Aall", ("Cs", sl)], [PSK[pi]])
                            for tt in range(NT):
                                mm(ps[pi][:, 0:SB], B_all[:, tt, cc * 128:(cc + 1) * 128], Ss[sl][:, tt, :], False, tt == NT - 1,
                                   ["W_Ball", ("Ss", sl)], [PSK[pi]])
                            if cc % 2 == 0:
                                dve(lambda e, cc=cc, pi=pi, s0_=s0_: e.tensor_scalar(out=yfT[:, cc, s0_:s0_ + SB], in0=ps[pi][:, 0:SB],
                                                                                       scalar1=bfT[:, cc:cc + 1], scalar2=None, op0=ALU.add),
                                    [PSK[pi], "bfT"], [("yfT", sbk)])
                            else:
                                act(lambda e, cc=cc, pi=pi, s0_=s0_: e.activation(out=yfT[:, cc, s0_:s0_ + SB], in_=ps[pi][:, 0:SB],
                                                                                    func=AF.Identity, bias=bfT[:, cc:cc + 1], scale=1.0),
                                    [PSK[pi], "bfT"], [("yfT", sbk)])
                s3 = s2
                if True:
                    w_o = sbt(s3, "w_o", [128, 8, D], BF16)
                    g1bc = sbt(s3, "g1bc", [128, D], F32)
                    xt = [sbt(s3, "xr%d" % i, [128, D], F32) for i in range(2)]
                    x1 = [sbt(s3, "x1_%d" % i, [128, D], F32) for i in range(2)]
                    x1n = [sbt(s3, "x1n%d" % i, [128, D], BF16) for i in range(2)]
                    h2T = [sbt(s3, "h2T%d" % i, [128, 8, 128], BF16) for i in range(2)]
                    st5 = [sbt(s3, "st5_%d" % i, [128, 4], F32) for i in range(2)]
                    P.op("pool", lambda e: e.dma_start(out=w_o[:], in_=I["w_out"].rearrange("(k p) c -> p k c", p=128)),
                         (), ["w_o"], dma="w_o")
                    for hh in range(2):
                        mm(ps[hh][:, :], sel4[:, b, :], modrow[:, 0, hh * 512:(hh + 1) * 512], True, True, ["sel4", "modrow"], [PSK[hh]])
                        act(lambda e, hh=hh: e.activation(out=g1bc[:, hh * 512:(hh + 1) * 512], in_=ps[hh][:, :], func=AF.Copy), [PSK[hh]], ["g1bc"])
                    for k in range(8):
                        eng = "pool" if k % 2 else "dve"
                        P.op(eng, lambda e, k=k: e.tensor_tensor(out=w_o[:, k, :], in0=w_o[:, k, :], in1=g1bc[:], op=ALU.mult),
                             ["w_o", "g1bc"], [("w_o", k)])
                    def p3x(i):
                        sl = i % 2
                        dma("sp", xt[sl][:], x[b, i * 128:(i + 1) * 128, :], w=[("xr", sl)], key=("xr", sl))

                    def p3a(i):
                        sl = i % 2
                        xs_ = i % 2
                        xk = ("xr", xs_)
                        for hh in range(2):
                            for k in range(8):
                                src = yfT if k < 4 else yaT
                                mm(ps[2 + hh][:, :], src[:, k % 4, i * 128:(i + 1) * 128], w_o[:, k, hh * 512:(hh + 1) * 512],
                                   k == 0, k == 7, [("yfT", i // 2), "yaT", ("w_o", k)], [PSK[2 + hh]])
                            dve(lambda e, hh=hh: e.tensor_tensor(out=x1[sl][:, hh * 512:(hh + 1) * 512], in0=ps[2 + hh][:, :],
                                                                 in1=xt[xs_][:, hh * 512:(hh + 1) * 512], op=ALU.add),
                                [PSK[2 + hh], xk], [("x1", sl)])
                        dma("sp", out[b, i * 128:(i + 1) * 128, :], x1[sl][:], r=[("x1", sl)], w=[("out", b)], key=("ox", sl))
                        s5 = st5[i % 2]
                        sk = ("st5", i % 2)
                        act(lambda e: e.activation(out=x1n[sl][:], in_=x1[sl][:], func=AF.Square, accum_out=s5[:, 0:1]),
                            [("x1", sl)], [("x1n", sl), sk])
                        act(lambda e: e.activation(out=s5[:, 1:2], in_=s5[:, 0:1], func=AF.Ln, bias=epsc[:], scale=1.0 / D),
                            [sk, "epsc"], [sk])
                        act(lambda e: e.activation(out=s5[:, 2:3], in_=s5[:, 1:2], func=AF.Exp, scale=-0.5), [sk], [sk])
                        act(lambda e: e.activation(out=x1n[sl][:], in_=x1[sl][:], func=AF.Identity, scale=s5[:, 2:3]),
                            [("x1", sl), sk], [("x1n", sl)])
                        dma("sp", h2s[b, i * 128:(i + 1) * 128, :], x1n[sl][:], r=[("x1n", sl)], w=[("h2s", b)], key=("oh", sl))

                    def p3b(i):
                        sl, s2 = i % 2, i % 2
                        for k in range(8):
                            mm(ps[k // 4][:, (k % 4) * 128:(k % 4 + 1) * 128], x1n[sl][:, k * 128:(k + 1) * 128], ident[:],
                               True, True, [("x1n", sl), "ident"], [PSK[k // 4]])
                        for k in range(8):
                            dve(lambda e, k=k: e.tensor_scalar(
                                out=h2T[s2][:, k, :], in0=ps[k // 4][:, (k % 4) * 128:(k % 4 + 1) * 128],
                                scalar1=colG2[:, b, k:k + 1], scalar2=colSH2[:, b, k:k + 1], op0=ALU.mult, op1=ALU.add),
                                [PSK[k // 4], "colG2", "colSH2"], [("h2T", s2)])
                        pr = 4 + (i % 2)
                        for k in range(8):
                            mm(ps[pr][0:64, 0:128], w_r4[:, k, b, :], h2T[s2][:, k, :], k == 0, k == 7, ["w_r4", ("h2T", s2)], [PSK[pr]])
                        if b == 0:
                            dve(lambda e: e.tensor_copy(out=logit[:, i * 128:(i + 1) * 128], in_=ps[pr][0:64, 0:128]), [PSK[pr]], [("logit", i)])
                        else:
                            dve(lambda e: e.tensor_tensor(out=logit[:, i * 128:(i + 1) * 128], in0=ps[pr][0:64, 0:128],
                                                          in1=logit[:, i * 128:(i + 1) * 128], op=ALU.add), [PSK[pr], ("logit", i)], [("logit", i)])

                    def p3_iter(i):
                        if i + 1 < NT:
                            p3x(i + 1)
                        if i < NT:
                            p3a(i)
                        if i >= 1:
                            p3b(i - 1)

                    p3x(0)
                    nsb = S // SB
                    for sbk in range(nsb):
                        dft(sbk)
                        if sbk >= 1:
                            p3_iter(2 * (sbk - 1))
                            p3_iter(2 * (sbk - 1) + 1)
                    for i in range(2 * (nsb - 1), NT + 1):
                        p3_iter(i)
                    if mode == "mixer" and b == 0:
                        dma("sp", dbg_yf, yfT[:], r=[("yfT", q_) for q_ in range(8)], w=["dbg_yf"], key="dbg1")
                        dma("sp", dbg_ya, yaT[:], r=["yaT"], w=["dbg_ya"], key="dbg2")
                    P.end_sched()
                    P.end_block()
                s2.close()

        stA.close()
        if mode == "mixer":
            stL.close()
            return nc

        sbw = ExitStack()
        wb = [sbt(sbw, "wb%d" % i, [128, 8, D], BF16) for i in range(NWB)]
        wsrc = (I["w_gate"], I["w_up"], I["w_down"])

        def load_w(m):
            e_, wh = m // 3, m % 3
            if e_ >= NE:
                return
            P.op("pool", lambda e: e.dma_start(out=wb[m % NWB][:], in_=wsrc[wh][e_].rearrange("(k p) c -> p k c", p=128)),
                 (), [("wb", m % NWB)], dma=("wb", m % NWB))

        for m in range(NWB):
            load_w(m)
        with ExitStack() as sr:
            expl = sbt(sr, "expl", [64, S], F32)
            wk = [sbt(sr, "wk%d" % i, [64, S], F32) for i in range(2)]
            vals = sbt(sr, "vals", [64, CAP], F32)
            idxs = sbt(sr, "idxs", [64, CAP], U32)
            idxf = sbt(sr, "idxf", [64, CAP], F32)
            idxTf = sbt(sr, "idxTf", [128, 2, 64], F32)
            act(lambda e: e.activation(out=expl[:], in_=logit[:], func=AF.Exp), [("logit", i) for i in range(NT)], ["expl"])
            for j in range(4):
                mm(ps[j][0:64, :], bd[:], expl[:, j * 512:(j + 1) * 512], True, True, ["bd", "expl"], [PSK[j]])
                dve(lambda e, j=j: e.reciprocal(out=wk[1][:, j * 512:(j + 1) * 512], in_=ps[j][0:64, :]), [PSK[j]], ["wk1"])
            dve(lambda e: e.tensor_tensor(out=wk[0][:], in0=expl[:], in1=wk[1][:], op=ALU.mult), ["expl", "wk1"], ["wk0"])
            cur = 0
            for it in range(CAP // 8):
                c8 = slice(8 * it, 8 * it + 8)
                ck = "wk%d" % cur
                dve(lambda e, cur=cur, c8=c8: e.max(out=vals[:, c8], in_=wk[cur][:]), [ck], ["vals"])
                dve(lambda e, cur=cur, c8=c8: e.max_index(out=idxs[:, c8], in_max=vals[:, c8], in_values=wk[cur][:]), [ck, "vals"], ["idxs"])
                if it < CAP // 8 - 1:
                    dve(lambda e, cur=cur, c8=c8: e.match_replace(out=wk[1 - cur][:], in_to_replace=vals[:, c8], in_values=wk[cur][:],
                                                                  imm_value=-1.0), [ck, "vals"], ["wk%d" % (1 - cur)])
                    cur = 1 - cur
            dve(lambda e: e.tensor_copy(out=idxf[:], in_=idxs[:]), ["idxs"], ["idxf"])
            for hf in range(2):
                mm(ps[4][:, hf * 64:(hf + 1) * 64], idxf[:, hf * 128:(hf + 1) * 128], identf[0:64, 0:64], True, True,
                   ["idxf", "identf"], [PSK[4]])
                mm(ps[5][:, hf * 64:(hf + 1) * 64], vals[:, hf * 128:(hf + 1) * 128], identf[0:64, 0:64], True, True,
                   ["vals", "identf"], [PSK[5]])
            dve(lambda e: e.tensor_scalar(out=idxTf[:], in0=ps[4][:, 0:128].rearrange("p (h c) -> p h c", h=2), scalar1=0.25, scalar2=None, op0=ALU.add), [PSK[4]], ["idxTf"])
            for b in range(1, NB):
                dve(lambda e, b=b: e.tensor_scalar(out=idxTf[:, :, b * 16:(b + 1) * 16], in0=idxTf[:, :, b * 16:(b + 1) * 16],
                                                   scalar1=float(b * S), scalar2=None, op0=ALU.add), ["idxTf"], ["idxTf"])
            dve(lambda e: e.tensor_copy(out=idxU[:], in_=idxTf[:]), ["idxTf"], ["idxU"])
            dve(lambda e: e.tensor_copy(out=valT[:], in_=ps[5][:, 0:128].rearrange("p (h c) -> p h c", h=2)), [PSK[5]], ["valT"])
            for b in range(NB):
                for hh in range(2):
                    pi = 6 + hh
                    mm(ps[pi][:, :], sel4[:, b, :], modrow[:, 1, hh * 512:(hh + 1) * 512], True, True, ["sel4", "modrow"], [PSK[pi]])
                    act(lambda e, b=b, hh=hh, pi=pi: e.activation(out=g2bc[:, b, hh * 512:(hh + 1) * 512], in_=ps[pi][:, :], func=AF.Copy),
                        [PSK[pi]], ["g2bc"])
            P.end_block()

        with ExitStack() as sb_:
            NXG = 8
            xg = [sbt(sb_, "xg%d" % i, [128, D], BF16) for i in range(NXG)]
            xinT2 = [sbt(sb_, "xinT%d" % i, [128, 8, 8 * 128], BF16) for i in range(2)]
            hTe = sbt(sb_, "hTe", [128, 8, 8 * 128], BF16)
            sg = [sbt(sb_, "sg%d" % i, [128, 512], F32) for i in range(2)]
            yv = [sbt(sb_, "yv%d" % i, [128, D], F32) for i in range(NYV)]
            def gathers(e_):
                for sti in range(8):
                    hf, b = sti // 4, sti % 4
                    col = b * 16 + e_
                    P.op("pool", lambda e, sti=sti, hf=hf, b=b, col=col: e.indirect_dma_start(
                        out=xg[sti][:], out_offset=None, in_=h2s_flat,
                        in_offset=bass.IndirectOffsetOnAxis(ap=idxU[:, hf, col:col + 1], axis=0)),
                        ["idxU", ("h2s", b)], [("xg", sti)], dma=("xg", sti))

            def transposes(e_, sti):
                xinT = xinT2[e_ % 2]
                xs = e_ % 2
                b = sti % 4
                for k in range(8):
                    mm(ps[k // 4][:, (k % 4) * 128:(k % 4 + 1) * 128], xg[sti][:, k * 128:(k + 1) * 128], ident[:],
                       True, True, [("xg", sti), "ident"], [PSK[k // 4]])
                for k in range(8):
                    if k < 4:
                        dve(lambda e, k=k: e.tensor_scalar(
                            out=xinT[:, k, sti * 128:(sti + 1) * 128], in0=ps[0][:, (k % 4) * 128:(k % 4 + 1) * 128],
                            scalar1=colG2[:, b, k:k + 1], scalar2=colSH2[:, b, k:k + 1], op0=ALU.mult, op1=ALU.add),
                            [PSK[0], "colG2", "colSH2"], [("xinT", xs, sti, 0)])
                    else:
                        act(lambda e, k=k: e.activation(
                            out=xinT[:, k, sti * 128:(sti + 1) * 128], in_=ps[1][:, (k % 4) * 128:(k % 4 + 1) * 128],
                            func=AF.Identity, bias=colSH2[:, b, k:k + 1], scale=colG2[:, b, k:k + 1]),
                            [PSK[1], "colG2", "colSH2"], [("xinT", xs, sti, 1)])

            gathers(0)
            for sti in range(8):
                transposes(0, sti)
            for e_ in range(NE):
                xinT = xinT2[e_ % 2]
                xs = e_ % 2
                mg, mu, md = 3 * e_, 3 * e_ + 1, 3 * e_ + 2
                wg, wu, wd_ = wb[mg % NWB], wb[mu % NWB], wb[md % NWB]
                if e_ + 1 < NE:
                    gathers(e_ + 1)
                for f in range(8):
                    for sh in range(2):
                        pg, pu = 2 + 2 * ((f * 2 + sh) % 2), 3 + 2 * ((f * 2 + sh) % 2)
                        rd = [("xinT", xs, s_, p_) for s_ in range(sh * 4, sh * 4 + 4) for p_ in range(2)]
                        for k in range(8):
                            mm(ps[pg][:, :], wg[:, k, f * 128:(f + 1) * 128], xinT[:, k, sh * 512:(sh + 1) * 512], k == 0, k == 7,
                               rd + [("wb", mg % NWB)], [PSK[pg]])
                        for k in range(8):
                            mm(ps[pu][:, :], wu[:, k, f * 128:(f + 1) * 128], xinT[:, k, sh * 512:(sh + 1) * 512], k == 0, k == 7,
                               rd + [("wb", mu % NWB)], [PSK[pu]])
                        si = (f * 2 + sh) % 2
                        act(lambda e, si=si, pg=pg: e.activation(out=sg[si][:], in_=ps[pg][:, :], func=AF.Silu), [PSK[pg]], [("sg", si)])
                        dve(lambda e, si=si, pu=pu, f=f, sh=sh: e.tensor_tensor(out=hTe[:, f, sh * 512:(sh + 1) * 512], in0=ps[pu][:, :],
                                                                                in1=sg[si][:], op=ALU.mult),
                            [PSK[pu], ("sg", si)], [("hTe", sh)])
                load_w(mg + NWB)
                load_w(mu + NWB)
                for sti in range(8):
                    hf, b = sti // 4, sti % 4
                    col = b * 16 + e_
                    yi = sti % NYV
                    for dh in range(2):
                        pi = 6 + dh
                        for f in range(8):
                            mm(ps[pi][:, :], hTe[:, f, sti * 128:(sti + 1) * 128], wd_[:, f, dh * 512:(dh + 1) * 512], f == 0, f == 7,
                               [("hTe", sti // 4), ("wb", md % NWB)], [PSK[pi]])
                        dve(lambda e, yi=yi, dh=dh, pi=pi, hf=hf, col=col, b=b: e.scalar_tensor_tensor(
                            out=yv[yi][:, dh * 512:(dh + 1) * 512], in0=ps[pi][:, :], scalar=valT[:, hf, col:col + 1],
                            in1=g2bc[:, b, dh * 512:(dh + 1) * 512], op0=ALU.mult, op1=ALU.mult),
                            [PSK[pi], "valT", "g2bc"], [("yv", yi)])
                    P.op("pool", lambda e, yi=yi, hf=hf, col=col, b=b: e.indirect_dma_start(
                        out=out_flat, out_offset=bass.IndirectOffsetOnAxis(ap=idxU[:, hf, col:col + 1], axis=0),
                        in_=yv[yi][:], in_offset=None, compute_op=ALU.add),
                        ["idxU", ("yv", yi), ("out", b)], [("out", b)], dma=("sc", yi))
                    if e_ + 1 < NE:
                        transposes(e_ + 1, sti)
                load_w(md + NWB)
            P.end_block()
        sbw.close()
        stL.close()
    return nc


_CACHE = {}


def _host_inputs(inp):
    f = np.float32
    g = lambda k: np.asarray(inp[k], dtype=f)
    x = g("x")
    c = g("c")
    w_in = g("w_in")[0]
    b_ada = g("b_ada")[0]
    shared = {
        "rel_bias": np.ascontiguousarray(g("rel_bias")),
        "w_ada": np.ascontiguousarray(g("w_ada")[0]),
        "b_adaT": np.ascontiguousarray(b_ada.reshape(48, 128).T),
        "b_ada_g": np.ascontiguousarray(np.stack([b_ada[2 * D:3 * D], b_ada[5 * D:6 * D]])),
        "gmixT": np.ascontiguousarray(g("norm_mix_g")[0].reshape(8, 128).T),
        "gffnT": np.ascontiguousarray(g("norm_ffn_g")[0].reshape(8, 128).T),
        "w_qkv": np.ascontiguousarray(w_in[:, 512:1280]),
        "w_in_fT": np.ascontiguousarray(w_in[:, 0:512].T),
        "w_fourier": np.ascontiguousarray(g("w_fourier")[0]),
        "bfT": np.ascontiguousarray(g("b_fourier")[0].reshape(4, 128).T),
        "gqk_row": np.ascontiguousarray(np.concatenate([np.tile(g("q_norm_g")[0], 8), np.tile(g("k_norm_g")[0], 2)])),
        "sink": np.ascontiguousarray(g("sink")[0]),
        "w_out": np.ascontiguousarray(g("w_out")[0]),
        "w_router": np.ascontiguousarray(g("w_router")[0]),
        "w_gate": np.ascontiguousarray(g("w_gate")[0]),
        "w_up": np.ascontiguousarray(g("w_up")[0]),
        "w_down": np.ascontiguousarray(g("w_down")[0]),
    }
    shared.update(_consts())
    maps = []
    for r in range(NCORES):
        m = dict(shared)
        m["x"] = np.ascontiguousarray(x[r * NB:(r + 1) * NB])
        m["cT"] = np.ascontiguousarray(c[r * NB:(r + 1) * NB].T)
        maps.append(m)
    return maps


def kernel(**inputs):
    if "nc" not in _CACHE:
        _CACHE["nc"] = build("full")
    nc = _CACHE["nc"]
    maps = _host_inputs(inputs)
    res = run_bass_kernel_spmd(nc, maps, core_ids=list(range(NCORES)))
    outs = [np.asarray(r["out"], dtype=np.float32) for r in res.results]
    return np.concatenate(outs, axis=0)
```

```python
from contextlib import ExitStack
import math
import numpy as np
import ml_dtypes
import concourse.bass as bass
import concourse.mybir as mybir
from concourse.bass_utils import run_bass_kernel_spmd

F32 = mybir.dt.float32
BF16 = mybir.dt.bfloat16
U32 = mybir.dt.uint32
ALU = mybir.AluOpType
AF = mybir.ActivationFunctionType
AX = mybir.AxisListType

NCORES = 8
NB = 4
S = 2048
D = 1024
NT = S // 128
NE = 16
CAP = 256
EPS = 1e-6
NWB = 5
NYV = 3


class Prog:
    ENG = ("pe", "act", "dve", "pool", "sp")

    def __init__(self, nc, stack):
        self.nc = nc
        self.stack = stack
        self.eng = {"pe": nc.tensor, "act": nc.scalar, "dve": nc.vector,
                    "pool": nc.gpsimd, "sp": nc.sync}
        self.ops = {k: [] for k in self.ENG}
        self.cnt = {k: 0 for k in self.ENG}
        self.esem = {k: stack.enter_context(nc.semaphore("es_" + k)) for k in self.ENG}
        self.waited = {k: {} for k in self.ENG}
        self.res_w = {}
        self.res_r = {}
        self.dsem = {}
        self.nops = 0

    def _dma_sem(self, key):
        if key not in self.dsem:
            s = self.stack.enter_context(self.nc.semaphore("ds_%d" % len(self.dsem)))
            self.dsem[key] = [s, 0]
        return self.dsem[key]

    class _Rec:
        def __init__(self):
            self.name, self.a, self.k = None, (), {}

        def __getattr__(self, name):
            def f(*a, **k):
                self.name, self.a, self.k = name, a, k
                return self
            return f

        def then_inc(self, *a, **k):
            return self

    def begin_sched(self):
        self.buf = []

    def _cost(self, engine, fn, dma):
        r = Prog._Rec()
        fn(r)
        def fsz(ap):
            n = 1
            for d in list(ap.shape)[1:]:
                n *= int(d)
            return n
        if dma is not None:
            ap = r.k.get("out", r.a[0] if r.a else None)
            nbytes = fsz(ap) * int(ap.shape[0]) * 4
            occ = 0.06 if engine != "pool" else 1.0
            return occ, 2.0 + nbytes / 120e3
        if engine == "pe":
            rhs = r.a[2] if len(r.a) > 2 else r.k.get("rhs")
            lhs = r.a[1] if len(r.a) > 1 else r.k.get("lhsT")
            n = fsz(rhs)
            mult = 4.0 if str(lhs.dtype).endswith("float32") else 1.0
            return mult * max(n, 64) / 2400.0 + 0.01, 0.06
        out = r.k.get("out", r.a[0] if r.a else None)
        n = fsz(out) if out is not None else 64
        if engine == "act":
            return 0.2 + n / 1200.0, 0.06
        if engine == "dve":
            return 0.07 + max(n, 64) / 960.0, 0.06
        return 0.3 + n / 500.0, 0.1

    def end_sched(self):
        buf, self.buf = self.buf, None
        n = len(buf)
        preds = [set() for _ in range(n)]
        last_w, readers = {}, {}
        for i, (engine, fn, reads, writes, dma) in enumerate(buf):
            for r in reads:
                if r in last_w:
                    preds[i].add(last_w[r])
                if isinstance(r, tuple) and r[0] == "ps":
                    for t in readers.get(r, ()):
                        if buf[t][0] != engine:
                            preds[i].add(t)
            for w in writes:
                if w in last_w:
                    preds[i].add(last_w[w])
                for t in readers.get(w, ()):
                    preds[i].add(t)
            for r in reads:
                readers.setdefault(r, []).append(i)
            for w in writes:
                last_w[w] = i
                readers[w] = []
            preds[i].discard(i)
        succs = [[] for _ in range(n)]
        npred = [len(p) for p in preds]
        for i, p in enumerate(preds):
            for j in p:
                succs[j].append(i)
        costs = [self._cost(b[0], b[1], b[4]) for b in buf]
        import heapq
        ready_t = [0.0] * n
        cand = {e: [] for e in self.ENG}
        for i in range(n):
            if npred[i] == 0:
                heapq.heappush(cand[buf[i][0]], (0.0, i))
        free_at = {e: 0.0 for e in self.ENG}
        order = []
        done = 0
        while done < n:
            best = None
            for e in self.ENG:
                h = cand[e]
                if not h:
                    continue
                bi = None
                for (rt, i) in h:
                    stt = max(rt, free_at[e])
                    key = (stt, i)
                    if bi is None or key < bi[0]:
                        bi = (key, rt, i)
                if best is None or bi[0] < best[0]:
                    best = (bi[0], bi[1], bi[2], e)
            (stt, _), rt, i, e = best
            cand[e].remove((rt, i))
            heapq.heapify(cand[e])
            occ, lat = costs[i]
            free_at[e] = stt + occ
            fin = stt + occ + lat
            order.append((stt, i))
            done += 1
            for j in succs[i]:
                ready_t[j] = max(ready_t[j], fin)
                npred[j] -= 1
                if npred[j] == 0:
                    heapq.heappush(cand[buf[j][0]], (ready_t[j], j))
        order.sort()
        self.sim_time = max(free_at.values())
        for _, i in order:
            engine, fn, reads, writes, dma = buf[i]
            self.op(engine, fn, reads, writes, dma)

    def op(self, engine, fn, reads=(), writes=(), dma=None):
        if getattr(self, "buf", None) is not None:
            self.buf.append((engine, fn, tuple(reads), tuple(writes), dma))
            return None
        deps = {}

        def need(tok, is_war=False):
            sem, val, e, d = tok
            if e == engine and dma is None and d is None:
                if engine == "pe" or is_war:
                    return
            k = id(sem)
            if k not in deps or deps[k][1] < val:
                deps[k] = (sem, val)

        for r in reads:
            if r in self.res_w:
                need(self.res_w[r])
            if isinstance(r, tuple) and r[0] == "ps":
                for t in self.res_r.get(r, ()):
                    if t[2] != engine:
                        need(t)
        for w in writes:
            if w in self.res_w:
                need(self.res_w[w])
            for t in self.res_r.get(w, ()):
                need(t, True)
        waits = []
        wd = self.waited[engine]
        for k, (sem, val) in deps.items():
            if wd.get(k, 0) >= val:
                continue
            wd[k] = val
            waits.append((sem, val))
        if dma is None:
            self.cnt[engine] += 1
            tok = (self.esem[engine], self.cnt[engine], engine, None)
            inc = 1
        else:
            d = self._dma_sem(dma)
            d[1] += 16
            tok = (d[0], d[1], engine, dma)
            inc = 16
        for r in reads:
            self.res_r.setdefault(r, []).append(tok)
        for w in writes:
            self.res_w[w] = tok
            self.res_r[w] = []
        eng = self.eng[engine]
        tsem = tok[0]

        def run():
            for sem, val in waits:
                eng.wait_ge(sem, val)
            fn(eng).then_inc(tsem, inc)

        self.ops[engine].append(run)
        self.nops += 1
        return tok

    def end_block(self):
        sp = self.eng["sp"]
        toks = [(d[0], d[1]) for d in self.dsem.values() if d[1] > 0]
        wd = self.waited["sp"]

        def run():
            for sem, val in toks:
                if wd.get(id(sem), 0) < val:
                    wd[id(sem)] = val
                    sp.wait_ge(sem, val)

        self.ops["sp"].append(run)
        ops = self.ops
        with self.nc.Block() as block:
            @block.tensor
            def _(e):
                for f in ops["pe"]:
                    f()

            @block.scalar
            def _(e):
                for f in ops["act"]:
                    f()

            @block.vector
            def _(e):
                for f in ops["dve"]:
                    f()

            @block.gpsimd
            def _(e):
                for f in ops["pool"]:
                    f()

            @block.sync
            def _(e):
                for f in ops["sp"]:
                    f()
        self.ops = {k: [] for k in self.ENG}


def _bucket_table():
    import jax
    import jax.numpy as jnp
    with jax.default_device(jax.devices("cpu")[0]):
        rel = jnp.arange(-255, 256, dtype=jnp.int32)
        half = 16
        max_exact = 8
        ret = jnp.where(rel > 0, half, 0)
        n = jnp.abs(rel)
        nf = jnp.maximum(n, 1).astype(jnp.float32)
        large = max_exact + (jnp.log(nf / max_exact) / math.log(128 / max_exact)
                             * (half - max_exact)).astype(jnp.int32)
        large = jnp.minimum(large, half - 1)
        return np.asarray(ret + jnp.where(n < max_exact, n, large)).astype(np.int64)


def _consts():
    c = {}
    c["ident"] = np.eye(128, dtype=np.float32).astype(ml_dtypes.bfloat16)
    c["jmat"] = np.eye(128, dtype=np.float32)[::-1].copy().astype(ml_dtypes.bfloat16)
    c["identf"] = np.eye(128, dtype=np.float32)
    t = np.arange(S, dtype=np.int64)
    ang = 2.0 * np.pi * ((t[:, None] * t[None, :]) % S).astype(np.float64) / S
    def lay(m):
        return np.ascontiguousarray(m.reshape(NT, 128, S // 256, 256).transpose(2, 1, 0, 3))
    c["cmat"] = lay(np.cos(ang).astype(np.float32).astype(ml_dtypes.bfloat16))
    c["smat"] = lay((-np.sin(ang)).astype(np.float32).astype(ml_dtypes.bfloat16))
    k = np.arange(64, dtype=np.int64)
    a64 = 2.0 * np.pi * ((k[:, None] * k[None, :]) % 64).astype(np.float64) / 64
    cc = np.cos(a64).astype(np.float32)
    sc = np.sin(a64).astype(np.float32)
    c["ccdup"] = np.concatenate([cc, cc], axis=1)
    c["scdup"] = np.concatenate([sc, sc], axis=1)
    bk = _bucket_table()
    ohr = np.zeros((32, 511), np.float32)
    maskr = np.zeros((8, 511), np.float32)
    for m in range(511):
        r = 255 - m
        if abs(r) <= 128:
            ohr[bk[r + 255], m] = 1.0
            maskr[:, m] = 1.0
    c["ohr"] = ohr
    c["maskr"] = maskr
    sel = np.zeros((4, 4, 128), np.float32)
    for b in range(4):
        sel[b, b, :] = 1.0
    c["sel4"] = sel.transpose(1, 0, 2).copy()
    bd = np.zeros((64, 64), np.float32)
    for b in range(4):
        bd[16 * b:16 * b + 16, 16 * b:16 * b + 16] = 1.0
    c["bd"] = bd
    return c


CONST_SPECS = [("ident", [128, 128], BF16), ("jmat", [128, 128], BF16), ("identf", [128, 128], F32),
               ("cmat", [S // 256, 128, NT, 256], BF16), ("smat", [S // 256, 128, NT, 256], BF16), ("ccdup", [64, 128], F32),
               ("scdup", [64, 128], F32), ("ohr", [32, 511], F32), ("maskr", [8, 511], F32),
               ("sel4", [4, 4, 128], F32), ("bd", [64, 64], F32)]

IN_SPECS = [("x", [NB, S, D]), ("cT", [D, NB]), ("rel_bias", [32, 8]), ("w_ada", [D, 6 * D]),
            ("b_adaT", [128, 48]), ("b_ada_g", [2, D]), ("gmixT", [128, 8]), ("gffnT", [128, 8]),
            ("w_qkv", [D, 768]), ("w_in_fT", [512, D]), ("w_fourier", [8, 64, 64]), ("bfT", [128, 4]),
            ("gqk_row", [640]), ("sink", [8]), ("w_out", [D, D]), ("w_router", [D, NE]),
            ("w_gate", [NE, D, D]), ("w_up", [NE, D, D]), ("w_down", [NE, D, D])]


def build(mode="full"):
    nc = bass.Bass("TRN2", target_bir_lowering=False)
    I = {}
    for name, shape in IN_SPECS:
        I[name] = nc.dram_tensor(name, shape, F32, kind="ExternalInput").ap()
    for name, shape, dt in CONST_SPECS:
        I[name] = nc.dram_tensor(name, shape, dt, kind="ExternalInput").ap()
    out = nc.dram_tensor("out", [NB, S, D], F32, kind="ExternalOutput").ap()
    h2s = nc.dram_tensor("h2s", [NB, S, D], BF16, kind="Internal").ap()
    wv_t = nc.dram_tensor("wv", [8, 511], BF16, kind="Internal")
    wv = wv_t.ap()
    x = I["x"]
    h2s_flat = h2s.rearrange("b s d -> (b s) d")
    out_flat = out.rearrange("b s d -> (b s) d")
    if mode == "mixer":
        dbg_yf = nc.dram_tensor("dbg_yf", [128, 4, S], BF16, kind="ExternalOutput").ap()
        dbg_ya = nc.dram_tensor("dbg_ya", [128, 4, S], BF16, kind="ExternalOutput").ap()

    with ExitStack() as st:
        P = Prog(nc, st)

        uid = [0]

        def sbt(stack, name, shape, dt):
            uid[0] += 1
            return stack.enter_context(nc.sbuf_tensor("sb%d_%s" % (uid[0], name), shape, dt))

        ps = [st.enter_context(nc.psum_tensor("ps%d" % i, [128, 512], F32)) for i in range(8)]
        PSK = [("ps", i) for i in range(8)]

        def dve(fn, r=(), w=()):
            return P.op("dve", fn, r, w)

        def act(fn, r=(), w=()):
            return P.op("act", fn, r, w)

        def pe(fn, r=(), w=()):
            return P.op("pe", fn, r, w)

        def pool(fn, r=(), w=()):
            return P.op("pool", fn, r, w)

        def dma(q, out_, in_, r=(), w=(), key=None, **kw):
            return P.op(q, lambda e: e.dma_start(out=out_, in_=in_, **kw), r, w, dma=key)

        def mm(o, l, rh, start, stop, r, w):
            return pe(lambda e: e.matmul(o, l, rh, start=start, stop=stop), r, w)

        stA = ExitStack()
        ident = sbt(st, "ident", [128, 128], BF16)
        jmat = sbt(st, "jmat", [128, 128], BF16)
        identf = sbt(st, "identf", [128, 128], F32)
        epsc = sbt(st, "epsc", [128, 1], F32)
        colG1 = sbt(st, "colG1", [128, NB, 8], F32)
        colSH1 = sbt(st, "colSH1", [128, NB, 8], F32)
        colG2 = sbt(st, "colG2", [128, NB, 8], F32)
        colSH2 = sbt(st, "colSH2", [128, NB, 8], F32)
        modrow = sbt(st, "modrow", [4, 2, D], F32)
        sel4 = sbt(st, "sel4", [4, 4, 128], F32)
        bd = sbt(st, "bd", [64, 64], F32)
        idxU = sbt(st, "idxU", [128, 2, 64], U32)
        valT = sbt(st, "valT", [128, 2, 64], F32)
        g2bc = sbt(st, "g2bc", [128, NB, D], F32)
        stL = ExitStack()
        logit = sbt(stL, "logit", [64, S], F32)
        W_A = sbt(stA, "W_A", [128, 8, 512], BF16)
        W_B = sbt(stA, "W_B", [128, 8, 512], BF16)
        expbT = sbt(stA, "expbT", [128, 8, 3, 128], BF16)
        gqk = sbt(stA, "gqk", [128, 640], F32)
        expsink = sbt(stA, "expsink", [128, 8], F32)
        bfT = sbt(stA, "bfT", [128, 4], F32)
        w_r4 = sbt(stA, "w_r4", [128, 8, 4, 64], BF16)

        dma("sp", ident[:], I["ident"], w=["ident"], key="c0")
        dma("sp", jmat[:], I["jmat"], w=["jmat"], key="c1")
        dma("sp", identf[:], I["identf"], w=["identf"], key="c2")
        dma("sp", sel4[:], I["sel4"], w=["sel4"], key="c3")
        dma("sp", bfT[:], I["bfT"], w=["bfT"], key="c4")
        dma("sp", bd[:], I["bd"], w=["bd"], key="c5")
        dma("sp", gqk[:], I["gqk_row"].partition_broadcast(128), w=["gqk"], key="c6")
        dve(lambda e: e.memset(epsc[:], EPS), w=["epsc"])

        with ExitStack() as s0:
            cT = sbt(s0, "cT", [128, 8, NB], F32)
            siluT = sbt(s0, "siluT", [128, 8, NB], F32)
            stmp = sbt(s0, "stmp", [128, 8, NB], F32)
            b_adaT = sbt(s0, "b_adaT", [128, 48], F32)
            gmixT = sbt(s0, "gmixT", [128, 8], F32)
            gffnT = sbt(s0, "gffnT", [128, 8], F32)
            modT = sbt(s0, "modT", [128, 48, NB], F32)
            badar = sbt(s0, "badar", [4, 2, D], F32)
            wa = [sbt(s0, "wa%d" % i, [128, 8, 512], BF16) for i in range(4)]
            siluTb = sbt(s0, "siluTb", [128, 8, NB], BF16)
            modrow_all = sbt(s0, "modrow_all", [4, 6 * D], F32)
            sinkb = sbt(s0, "sinkb", [128, 8], F32)
            rb = sbt(s0, "rb", [32, 8], F32)
            ohr = sbt(s0, "ohr", [32, 511], F32)
            maskr = sbt(s0, "maskr", [8, 511], F32)
            ev = sbt(s0, "ev", [8, 511], F32)
            wvs = sbt(s0, "wvs", [8, 511], BF16)
            wf = sbt(s0, "wf", [64, 8, 64], F32)
            ccdup = sbt(s0, "ccdup", [64, 128], F32)
            scdup = sbt(s0, "scdup", [64, 128], F32)
            winfT = sbt(s0, "winfT", [128, 4, D], F32)
            bdA = sbt(s0, "bdA", [128, 4, 128], F32)
            bdB = sbt(s0, "bdB", [128, 4, 128], F32)
            wr_f = sbt(s0, "wr_f", [128, 8, NE], F32)

            dma("sp", cT[:], I["cT"].rearrange("(k p) b -> p k b", p=128), w=["cT"], key="s0")
            dma("sp", b_adaT[:], I["b_adaT"], w=["b_adaT"], key="s1")
            dma("sp", gmixT[:], I["gmixT"], w=["gmixT"], key="s2")
            dma("sp", gffnT[:], I["gffnT"], w=["gffnT"], key="s3")
            for j in range(2):
                dma("sp", badar[:, j, :], I["b_ada_g"][j].partition_broadcast(4), w=["badar%d" % j], key="s4%d" % j)
            dma("sp", sinkb[:], I["sink"].partition_broadcast(128), w=["sinkb"], key="s5")
            dma("sp", rb[:], I["rel_bias"], w=["rb"], key="s6")
            dma("sp", ohr[:], I["ohr"], w=["ohr"], key="s7")
            dma("sp", maskr[:], I["maskr"], w=["maskr"], key="s8")
            dma("sp", wf[:], I["w_fourier"].rearrange("g k d -> k g d"), w=["wf"], key="s9")
            dma("sp", ccdup[:], I["ccdup"], w=["ccdup"], key="s10")
            dma("sp", scdup[:], I["scdup"], w=["scdup"], key="s11")
            dma("sp", winfT[:], I["w_in_fT"].rearrange("(j p) m -> p j m", p=128), w=["winfT"], key="s12")
            dma("sp", wr_f[:], I["w_router"].rearrange("(k p) e -> p k e", p=128), w=["wr_f"], key="s13")

            act(lambda e: e.activation(out=stmp[:], in_=cT[:], func=AF.Exp, scale=-1.0), ["cT"], ["stmp"])
            dve(lambda e: e.tensor_scalar(out=stmp[:], in0=stmp[:], scalar1=1.0, scalar2=None, op0=ALU.add), ["stmp"], ["stmp"])
            dve(lambda e: e.reciprocal(out=stmp[:], in_=stmp[:]), ["stmp"], ["stmp"])
            dve(lambda e: e.tensor_tensor(out=siluT[:], in0=cT[:], in1=stmp[:], op=ALU.mult), ["cT", "stmp"], ["siluT"])

            dve(lambda e: e.tensor_copy(out=siluTb[:], in_=siluT[:]), ["siluT"], ["siluTb"])
            for pc in range(12):
                a, hh = pc // 2, pc % 2
                wsl = wa[pc % 4]
                wk = "wa%d" % (pc % 4)
                P.op("pool", lambda e, wsl=wsl, pc=pc: e.dma_start(
                    out=wsl[:], in_=I["w_ada"][:, pc * 512:(pc + 1) * 512].rearrange("(k p) e -> p k e", p=128)),
                    (), [wk], dma=wk)
                pr = pc % 2
                for k in range(8):
                    mm(ps[pr][0:4, :], siluTb[:, k, :], wsl[:, k, :], k == 0, k == 7, [wk, "siluTb"], [PSK[pr]])
                dve(lambda e, pc=pc, pr=pr: e.tensor_copy(out=modrow_all[:, pc * 512:(pc + 1) * 512], in_=ps[pr][0:4, :]),
                    [PSK[pr]], [("mra", pc)])
                if a in (2, 5):
                    gi = 0 if a == 2 else 1
                    dve(lambda e, gi=gi, hh=hh, pc=pc: e.tensor_tensor(out=modrow[:, gi, hh * 512:(hh + 1) * 512],
                                                                       in0=modrow_all[:, pc * 512:(pc + 1) * 512],
                                                                       in1=badar[:, gi, hh * 512:(hh + 1) * 512], op=ALU.add),
                        [("mra", pc), "badar%d" % gi], ["modrow"])
            psM = ps[2]
            for j in range(48):
                mm(psM[:, j * 4:j * 4 + 4], modrow_all[:, j * 128:(j + 1) * 128], identf[0:4, 0:4], True, True,
                   [("mra", j // 4), "identf"], [PSK[2]])
            dve(lambda e: e.tensor_tensor(out=modT[:], in0=psM[:, 0:192].rearrange("p (j b) -> p j b", b=NB),
                                          in1=b_adaT[:].unsqueeze(2).to_broadcast([128, 48, NB]), op=ALU.add),
                [PSK[2], "b_adaT"], ["modT"])
            for b in range(NB):
                dve(lambda e, b=b: e.tensor_copy(out=colSH1[:, b, :], in_=modT[:, 0:8, b]), ["modT"], ["colSH1"])
                dve(lambda e, b=b: e.scalar_tensor_tensor(out=colG1[:, b, :], in0=modT[:, 8:16, b], scalar=1.0, in1=gmixT[:],
                                                          op0=ALU.add, op1=ALU.mult), ["modT", "gmixT"], ["colG1"])
                dve(lambda e, b=b: e.tensor_copy(out=colSH2[:, b, :], in_=modT[:, 24:32, b]), ["modT"], ["colSH2"])
                dve(lambda e, b=b: e.scalar_tensor_tensor(out=colG2[:, b, :], in0=modT[:, 32:40, b], scalar=1.0, in1=gffnT[:],
                                                          op0=ALU.add, op1=ALU.mult), ["modT", "gffnT"], ["colG2"])

            mm(ps[2][0:8, 0:511], rb[:], ohr[:], True, True, ["rb", "ohr"], [PSK[2]])
            act(lambda e: e.activation(out=ev[:], in_=ps[2][0:8, 0:511], func=AF.Exp), [PSK[2]], ["ev"])
            dve(lambda e: e.tensor_tensor(out=wvs[:], in0=ev[:], in1=maskr[:], op=ALU.mult), ["ev", "maskr"], ["wvs"])
            dma("sp", wv, wvs[:], r=["wvs"], w=["wv"], key="wv")
            for kt in range(3):
                src = bass.AP(tensor=wv_t, offset=256 - 128 * kt, ap=[[1, 128], [511, 8], [1, 128]])
                dma("sp", expbT[:, :, kt, :], src, r=["wv"], w=["expbT%d" % kt], key="eb%d" % kt)
            act(lambda e: e.activation(out=expsink[:], in_=sinkb[:], func=AF.Exp), ["sinkb"], ["expsink"])

            fscale = 1.0 / math.sqrt(S * 64.0)
            for (cd, cdk, bdX, bdk, psi) in ((ccdup, "ccdup", bdA, "bdA", 3), (scdup, "scdup", bdB, "bdB", 4)):
                for g in range(8):
                    mm(ps[psi][:, g * 64:(g + 1) * 64], cd[:], wf[:, g, :], True, True, [cdk, "wf"], [PSK[psi]])
                pool(lambda e, bdX=bdX: e.memset(bdX[:], 0.0), w=[bdk])
                for j in range(4):
                    dve(lambda e, bdX=bdX, psi=psi, j=j: e.tensor_scalar(
                        out=bdX[0:64, j, 0:64], in0=ps[psi][0:64, (2 * j) * 64:(2 * j + 1) * 64],
                        scalar1=fscale, scalar2=None, op0=ALU.mult), [PSK[psi], bdk], [bdk])
                    dve(lambda e, bdX=bdX, psi=psi, j=j: e.tensor_scalar(
                        out=bdX[64:128, j, 64:128], in0=ps[psi][64:128, (2 * j + 1) * 64:(2 * j + 2) * 64],
                        scalar1=fscale, scalar2=None, op0=ALU.mult), [PSK[psi], bdk], [bdk])
            for (bdX, bdk, WX, wk2, pbase) in ((bdA, "bdA", W_A, "W_A", 5), (bdB, "bdB", W_B, "W_B", 6)):
                for m in range(8):
                    for j in range(4):
                        mm(ps[pbase][:, j * 128:(j + 1) * 128], winfT[:, j, m * 128:(m + 1) * 128], bdX[:, j, :],
                           True, True, ["winfT", bdk], [PSK[pbase]])
                    act(lambda e, WX=WX, m=m, pbase=pbase: e.activation(out=WX[:, m, :], in_=ps[pbase][:, :], func=AF.Copy),
                        [PSK[pbase]], [wk2])

            pool(lambda e: e.memset(w_r4[:], 0.0), w=["w_r4"])
            for b in range(NB):
                dve(lambda e, b=b: e.tensor_copy(out=w_r4[:, :, b, 16 * b:16 * b + 16], in_=wr_f[:]), ["wr_f", "w_r4"], ["w_r4"])
            P.end_block()

        for b in range(NB):
            with ExitStack() as sa:
                A_all = sbt(sa, "A_all", [128, NT, 512], BF16)
                B_all = sbt(sa, "B_all", [128, NT, 512], BF16)
                yfT = sbt(sa, "yfT", [128, 4, S], BF16)
                yaT = sbt(sa, "yaT", [128, 4, S], BF16)
                with ExitStack() as s1:
                    NXT, NQ, NK, NV = 3, 4, 5, 8
                    w_qkv = sbt(s1, "w_qkv", [128, 8, 768], BF16)
                    xt = [sbt(s1, "xt%d" % i, [128, D], F32) for i in range(NXT)]
                    xn = [sbt(s1, "xn%d" % i, [128, D], BF16) for i in range(2)]
                    hT = [sbt(s1, "hT%d" % i, [128, 8, 128], BF16) for i in range(2)]
                    st4 = [sbt(s1, "st4_%d" % i, [128, 4], F32) for i in range(2)]
                    qkf = [sbt(s1, "qkf%d" % i, [128, 640], F32) for i in range(2)]
                    sq = [sbt(s1, "sq%d" % i, [128, 640], F32) for i in range(2)]
                    ssq = [sbt(s1, "ssq%d" % i, [128, 10], F32) for i in range(2)]
                    rq = [sbt(s1, "rq%d" % i, [128, 10], F32) for i in range(2)]
                    qkn = [sbt(s1, "qkn%d" % i, [128, 640], BF16) for i in range(2)]
                    vtmp = [sbt(s1, "vtmp%d" % i, [128, 128], BF16) for i in range(2)]
                    qT = [sbt(s1, "qT%d" % i, [64, 8 * 128], BF16) for i in range(NQ)]
                    kT = [sbt(s1, "kT%d" % i, [64, 2 * 128], BF16) for i in range(NK)]
                    Vr = [sbt(s1, "Vr%d" % i, [128, 2 * 65], BF16) for i in range(NV)]
                    Eb = [sbt(s1, "Eb%d" % i, [128, 512], BF16) for i in range(6)]
                    PT = [sbt(s1, "PT%d" % i, [128, 512], BF16) for i in range(12)]
                    den = [sbt(s1, "den%d" % i, [128, 4], F32) for i in range(4)]
                    ya = [sbt(s1, "ya%d" % i, [128, 512], BF16) for i in range(2)]
                    psb = [p_[:].bitcast(BF16) for p_ in ps]

                    P.begin_sched()
                    P.op("pool", lambda e: e.dma_start(out=w_qkv[:], in_=I["w_qkv"].rearrange("(k p) c -> p k c", p=128)),
                         (), ["w_qkv"], dma="wqkv")
                    for i in range(NV):
                        pool(lambda e, i=i: e.memset(Vr[i][:], 1.0), w=[("Vr", i)])
                    ecnt = [0]

                    def S0(i):
                        sl = i % NXT
                        dma("sp", xt[sl][:], x[b, i * 128:(i + 1) * 128, :], w=[("xt", sl)], key=("xt", sl))

                    def S1(i):
                        sl, s2 = i % NXT, i % 2
                        xk, nk, sk = ("xt", sl), ("xn", s2), ("st4", s2)
                        s4 = st4[s2]
                        act(lambda e: e.activation(out=xn[s2][:], in_=xt[sl][:], func=AF.Square, accum_out=s4[:, 0:1]),
                            [xk], [nk, sk])
                        act(lambda e: e.activation(out=s4[:, 1:2], in_=s4[:, 0:1], func=AF.Ln, bias=epsc[:], scale=1.0 / D),
                            [sk, "epsc"], [sk])
                        act(lambda e: e.activation(out=s4[:, 2:3], in_=s4[:, 1:2], func=AF.Exp, scale=-0.5), [sk], [sk])
                        act(lambda e: e.activation(out=xn[s2][:], in_=xt[sl][:], func=AF.Identity, scale=s4[:, 2:3]),
                            [xk, sk], [nk])

                    def S2(i):
                        s2 = i % 2
                        nk = ("xn", s2)
                        for k in range(8):
                            mm(ps[k // 4][:, (k % 4) * 128:(k % 4 + 1) * 128], xn[s2][:, k * 128:(k + 1) * 128], ident[:],
                               True, True, [nk, "ident"], [PSK[k // 4]])
                        for k in range(8):
                            if k < 8:
                                dve(lambda e, k=k: e.tensor_scalar(
                                    out=hT[s2][:, k, :], in0=ps[k // 4][:, (k % 4) * 128:(k % 4 + 1) * 128],
                                    scalar1=colG1[:, b, k:k + 1], scalar2=colSH1[:, b, k:k + 1], op0=ALU.mult, op1=ALU.add),
                                    [PSK[k // 4], "colG1", "colSH1"], [("hT", s2, 0)])
                            else:
                                act(lambda e, k=k: e.activation(
                                    out=hT[s2][:, k, :], in_=ps[1][:, (k % 4) * 128:(k % 4 + 1) * 128], func=AF.Identity,
                                    bias=colSH1[:, b, k:k + 1], scale=colG1[:, b, k:k + 1]),
                                    [PSK[1], "colG1", "colSH1"], [("hT", s2, 1)])

                    def S3(i):
                        s2 = i % 2
                        hk = [("hT", s2, 0), ("hT", s2, 1)]
                        hTs = hT[s2]
                        for k in range(8):
                            mm(ps[2][:, :], hTs[:, k, :], W_A[:, k, :], k == 0, k == 7, hk + ["W_A"], [PSK[2]])
                        act(lambda e: e.activation(out=A_all[:, i, :], in_=ps[2][:, :], func=AF.Copy), [PSK[2]], ["W_Aall"])
                        for k in range(8):
                            mm(ps[3][:, :], hTs[:, k, :], W_B[:, k, :], k == 0, k == 7, hk + ["W_B"], [PSK[3]])
                        dve(lambda e: e.tensor_copy(out=B_all[:, i, :], in_=ps[3][:, :]), [PSK[3]], ["W_Ball"])
                        for k in range(8):
                            mm(ps[2][:, :], hTs[:, k, :], w_qkv[:, k, 0:512], k == 0, k == 7, hk + ["w_qkv"], [PSK[2]])
                        for k in range(8):
                            mm(ps[3][:, 0:256], hTs[:, k, :], w_qkv[:, k, 512:768], k == 0, k == 7, hk + ["w_qkv"], [PSK[3]])
                        qf, vt = qkf[s2], vtmp[s2]
                        kq, kvt = ("qkf", s2), ("vtmp", s2)
                        act(lambda e: e.activation(out=qf[:, 0:512], in_=ps[2][:, :], func=AF.Copy), [PSK[2]], [kq])
                        dve(lambda e: e.tensor_copy(out=vt[:], in_=ps[3][:, 128:256]), [PSK[3]], [kvt])
                        dve(lambda e: e.tensor_copy(out=qf[:, 512:640], in_=ps[3][:, 0:128]), [PSK[3], kq], [kq])
                        mm(ps[3][:, 256:384], jmat[:], vt[:], True, True, ["jmat", kvt], [PSK[3]])
                        vk = ("Vr", i % NV)
                        dve(lambda e: e.tensor_copy(out=Vr[i % NV][:].rearrange("p (g c) -> p g c", g=2)[:, :, 0:64],
                                                    in_=ps[3][:, 256:384].rearrange("p (g c) -> p g c", g=2)),
                            [PSK[3]], [vk])

                    def S4(i):
                        s2 = i % 2
                        qf, sqq, ss_, rq_, qn = qkf[s2], sq[s2], ssq[s2], rq[s2], qkn[s2]
                        kq, ks, kss, krq, kqn = ("qkf", s2), ("sq", s2), ("ssq", s2), ("rq", s2), ("qkn", s2)
                        pool(lambda e: e.tensor_tensor(out=sqq[:], in0=qf[:], in1=qf[:], op=ALU.mult), [kq], [ks])
                        dve(lambda e: e.tensor_reduce(out=ss_[:], in_=sqq[:].rearrange("p (h c) -> p h c", h=10), axis=AX.X, op=ALU.add),
                            [ks], [kss])
                        act(lambda e: e.activation(out=rq_[:], in_=ss_[:], func=AF.Ln, bias=epsc[:], scale=1.0 / 64), [kss, "epsc"], [krq])
                        act(lambda e: e.activation(out=rq_[:], in_=rq_[:], func=AF.Exp, scale=-0.5), [krq], [krq])
                        pool(lambda e: e.tensor_tensor(out=sqq[:].rearrange("p (h c) -> p h c", h=10), in0=qf[:].rearrange("p (h c) -> p h c", h=10),
                                                       in1=rq_[:].unsqueeze(2).to_broadcast([128, 10, 64]), op=ALU.mult),
                             [kq, krq, ks], [ks])
                        pool(lambda e: e.tensor_tensor(out=qn[:], in0=sqq[:], in1=gqk[:], op=ALU.mult), [ks, "gqk"], [kqn])

                    def S5(i):
                        s2 = i % 2
                        qn, kqn = qkn[s2], ("qkn", s2)
                        qk_ = ("qT", i % NQ)
                        for h in range(8):
                            pb = 4 + (h // 4)
                            mm(ps[pb][0:64, (h % 4) * 128:(h % 4 + 1) * 128], qn[:, h * 64:(h + 1) * 64], ident[:],
                               True, True, [kqn, "ident"], [PSK[pb]])
                        for hb in range(2):
                            act(lambda e, hb=hb: e.activation(out=qT[i % NQ][:, hb * 512:(hb + 1) * 512], in_=ps[4 + hb][0:64, :], func=AF.Copy),
                                [PSK[4 + hb]], [qk_])
                        for g in range(2):
                            mm(ps[7][0:64, g * 128:(g + 1) * 128], qn[:, 512 + g * 64:512 + (g + 1) * 64], jmat[:],
                               True, True, [kqn, "jmat"], [PSK[7]])
                        act(lambda e: e.activation(out=kT[i % NK][:], in_=ps[7][0:64, 0:256], func=AF.Copy), [PSK[7]], [("kT", i % NK)])

                    def S6(ib):
                        qs = ib % NQ
                        kts = [kt for kt in (ib - 1, ib, ib + 1) if 0 <= kt < NT]
                        for g in range(2):
                            for kt in kts:
                                kidx = kt - ib + 1
                                ei = ecnt[0] % 6
                                ecnt[0] += 1
                                pt = (ib % 2) * 6 + g * 3 + kidx
                                psS = ps[4 + (ei % 2)]
                                psk = PSK[4 + (ei % 2)]
                                mm(psS[:, :], kT[kt % NK][:, g * 128:(g + 1) * 128], qT[qs][:, g * 512:(g + 1) * 512],
                                   True, True, [("kT", kt % NK), ("qT", qs)], [psk])
                                act(lambda e, psS=psS, ei=ei: e.activation(out=Eb[ei][:], in_=psS[:, :], func=AF.Exp, scale=0.125),
                                    [psk], [("Eb", ei)])
                                dve(lambda e, kidx=kidx, g=g, ei=ei, pt=pt: e.tensor_tensor(
                                    out=PT[pt][:].rearrange("p (h q) -> p h q", h=4), in0=Eb[ei][:].rearrange("p (h q) -> p h q", h=4),
                                    in1=expbT[:, 4 * g:4 * g + 4, kidx, :], op=ALU.mult),
                                    [("Eb", ei), "expbT%d" % kidx], [("PT", pt)])

                    def S7(ib):
                        kts = [kt for kt in (ib - 1, ib, ib + 1) if 0 <= kt < NT]
                        yk = ("ya", ib % 2)
                        for g in range(2):
                            psO = ps[6]
                            for hh in range(4):
                                for n, kt in enumerate(kts):
                                    kidx = kt - ib + 1
                                    pt = (ib % 2) * 6 + g * 3 + kidx
                                    mm(psO[:, hh * 65:(hh + 1) * 65], PT[pt][:, hh * 128:(hh + 1) * 128],
                                       Vr[kt % NV][:, g * 65:(g + 1) * 65], n == 0, n == len(kts) - 1,
                                       [("PT", pt), ("Vr", kt % NV)], [PSK[6]])
                            o3 = psO[:, 0:260].rearrange("p (h c) -> p h c", h=4)
                            di = (ib % 2) * 2 + g
                            dn = den[di]
                            dk = ("den", di)
                            dve(lambda e, o3=o3, g=g, dn=dn: e.tensor_tensor(out=dn[:], in0=o3[:, :, 64], in1=expsink[:, 4 * g:4 * g + 4], op=ALU.add),
                                [PSK[6], "expsink"], [dk])
                            dve(lambda e, dn=dn: e.reciprocal(out=dn[:], in_=dn[:]), [dk], [dk])
                            dve(lambda e, o3=o3, g=g, dn=dn: e.tensor_tensor(
                                out=ya[ib % 2][:, g * 256:(g + 1) * 256].rearrange("p (h c) -> p h c", h=4), in0=o3[:, :, 0:64],
                                in1=dn[:].unsqueeze(2).to_broadcast([128, 4, 64]), op=ALU.mult),
                                [PSK[6], dk], [yk])

                    def S8(ib):
                        yk = ("ya", ib % 2)
                        for cc in range(4):
                            mm(ps[7][:, cc * 128:(cc + 1) * 128], ya[ib % 2][:, cc * 128:(cc + 1) * 128], ident[:],
                               True, True, [yk, "ident"], [PSK[7]])
                        act(lambda e: e.activation(out=yaT[:, :, ib * 128:(ib + 1) * 128],
                                                   in_=ps[7][:, :].rearrange("p (c t) -> p c t", c=4), func=AF.Copy),
                            [PSK[7]], ["yaT"])

                    ok = lambda t: 0 <= t < NT
                    S0(0)
                    for j in range(NT + 9):
                        if ok(j + 1):
                            S0(j + 1)
                        if ok(j):
                            S1(j)
                        if ok(j - 6):
                            S6(j - 6)
                        if ok(j - 1):
                            S2(j - 1)
                        if ok(j - 7):
                            S7(j - 7)
                        if ok(j - 2):
                            S3(j - 2)
                        if ok(j - 3):
                            S4(j - 3)
                        if ok(j - 4):
                            S5(j - 4)
                        if ok(j - 8):
                            S8(j - 8)
                    P.end_sched()
                    P.end_block()
                s2 = ExitStack()
                if True:
                    SB = 256
                    Cs = [sbt(s2, "Cs%d" % i, [128, NT, SB], BF16) for i in range(2)]
                    Ss = [sbt(s2, "Ss%d" % i, [128, NT, SB], BF16) for i in range(2)]
                    P.begin_sched()

                    def dft(sbk):
                        sl = sbk % 2
                        s0_ = sbk * SB
                        dma("sp", Cs[sl][:], I["cmat"][sbk], w=[("Cs", sl)], key=("Cs", sl))
                        dma("sp", Ss[sl][:], I["smat"][sbk], w=[("Ss", sl)], key=("Ss", sl))
                        for cc in range(4):
                            pi = 6 + ((sbk * 4 + cc) % 2)
                            for tt in range(NT):
                                mm(ps[pi][:, 0:SB], A_all[:, tt, cc * 128:(cc + 1) * 128], Cs[sl][:, tt, :], tt == 0, False,
                                   ["W_Aall", ("Cs", sl)], [PSK[pi]])
                            for tt in range(NT):
                                mm(ps[pi][:, 0:SB], B_all[:, tt, cc * 128:(cc + 1) * 128], Ss[sl][:, tt, :], False, tt == NT - 1,
                                   ["W_Ball", ("Ss", sl)], [PSK[pi]])
                            if cc % 2 == 0:
                                dve(lambda e, cc=cc, pi=pi, s0_=s0_: e.tensor_scalar(out=yfT[:, cc, s0_:s0_ + SB], in0=ps[pi][:, 0:SB],
                                                                                       scalar1=bfT[:, cc:cc + 1], scalar2=None, op0=ALU.add),
                                    [PSK[pi], "bfT"], [("yfT", sbk)])
                            else:
                                act(lambda e, cc=cc, pi=pi, s0_=s0_: e.activation(out=yfT[:, cc, s0_:s0_ + SB], in_=ps[pi][:, 0:SB],
                                                                                    func=AF.Identity, bias=bfT[:, cc:cc + 1], scale=1.0),
                                    [PSK[pi], "bfT"], [("yfT", sbk)])
                s3 = s2
                if True:
                    w_o = sbt(s3, "w_o", [128, 8, D], BF16)
                    g1bc = sbt(s3, "g1bc", [128, D], F32)
                    xt = [sbt(s3, "xr%d" % i, [128, D], F32) for i in range(2)]
                    x1 = [sbt(s3, "x1_%d" % i, [128, D], F32) for i in range(2)]
                    x1n = [sbt(s3, "x1n%d" % i, [128, D], BF16) for i in range(2)]
                    h2T = [sbt(s3, "h2T%d" % i, [128, 8, 128], BF16) for i in range(2)]
                    st5 = [sbt(s3, "st5_%d" % i, [128, 4], F32) for i in range(2)]
                    P.op("pool", lambda e: e.dma_start(out=w_o[:], in_=I["w_out"].rearrange("(k p) c -> p k c", p=128)),
                         (), ["w_o"], dma="w_o")
                    for hh in range(2):
                        mm(ps[hh][:, :], sel4[:, b, :], modrow[:, 0, hh * 512:(hh + 1) * 512], True, True, ["sel4", "modrow"], [PSK[hh]])
                        act(lambda e, hh=hh: e.activation(out=g1bc[:, hh * 512:(hh + 1) * 512], in_=ps[hh][:, :], func=AF.Copy), [PSK[hh]], ["g1bc"])
                    for k in range(8):
                        eng = "pool" if k % 2 else "dve"
                        P.op(eng, lambda e, k=k: e.tensor_tensor(out=w_o[:, k, :], in0=w_o[:, k, :], in1=g1bc[:], op=ALU.mult),
                             ["w_o", "g1bc"], [("w_o", k)])
                    def p3x(i):
                        sl = i % 2
                        dma("sp", xt[sl][:], x[b, i * 128:(i + 1) * 128, :], w=[("xr", sl)], key=("xr", sl))

                    def p3a(i):
                        sl = i % 2
                        xs_ = i % 2
                        xk = ("xr", xs_)
                        for hh in range(2):
                            for k in range(8):
                                src = yfT if k < 4 else yaT
                                mm(ps[2 + hh][:, :], src[:, k % 4, i * 128:(i + 1) * 128], w_o[:, k, hh * 512:(hh + 1) * 512],
                                   k == 0, k == 7, [("yfT", i // 2), "yaT", ("w_o", k)], [PSK[2 + hh]])
                            dve(lambda e, hh=hh: e.tensor_tensor(out=x1[sl][:, hh * 512:(hh + 1) * 512], in0=ps[2 + hh][:, :],
                                                                 in1=xt[xs_][:, hh * 512:(hh + 1) * 512], op=ALU.add),
                                [PSK[2 + hh], xk], [("x1", sl)])
                        dma("sp", out[b, i * 128:(i + 1) * 128, :], x1[sl][:], r=[("x1", sl)], w=[("out", b)], key=("ox", sl))
                        s5 = st5[i % 2]
                        sk = ("st5", i % 2)
                        act(lambda e: e.activation(out=x1n[sl][:], in_=x1[sl][:], func=AF.Square, accum_out=s5[:, 0:1]),
                            [("x1", sl)], [("x1n", sl), sk])
                        act(lambda e: e.activation(out=s5[:, 1:2], in_=s5[:, 0:1], func=AF.Ln, bias=epsc[:], scale=1.0 / D),
                            [sk, "epsc"], [sk])
                        act(lambda e: e.activation(out=s5[:, 2:3], in_=s5[:, 1:2], func=AF.Exp, scale=-0.5), [sk], [sk])
                        act(lambda e: e.activation(out=x1n[sl][:], in_=x1[sl][:], func=AF.Identity, scale=s5[:, 2:3]),
                            [("x1", sl), sk], [("x1n", sl)])
                        dma("sp", h2s[b, i * 128:(i + 1) * 128, :], x1n[sl][:], r=[("x1n", sl)], w=[("h2s", b)], key=("oh", sl))

                    def p3b(i):
                        sl, s2 = i % 2, i % 2
                        for k in range(8):
                            mm(ps[k // 4][:, (k % 4) * 128:(k % 4 + 1) * 128], x1n[sl][:, k * 128:(k + 1) * 128], ident[:],
                               True, True, [("x1n", sl), "ident"], [PSK[k // 4]])
                        for k in range(8):
                            dve(lambda e, k=k: e.tensor_scalar(
                                out=h2T[s2][:, k, :], in0=ps[k // 4][:, (k % 4) * 128:(k % 4 + 1) * 128],
                                scalar1=colG2[:, b, k:k + 1], scalar2=colSH2[:, b, k:k + 1], op0=ALU.mult, op1=ALU.add),
                                [PSK[k // 4], "colG2", "colSH2"], [("h2T", s2)])
                        pr = 4 + (i % 2)
                        for k in range(8):
                            mm(ps[pr][0:64, 0:128], w_r4[:, k, b, :], h2T[s2][:, k, :], k == 0, k == 7, ["w_r4", ("h2T", s2)], [PSK[pr]])
                        if b == 0:
                            dve(lambda e: e.tensor_copy(out=logit[:, i * 128:(i + 1) * 128], in_=ps[pr][0:64, 0:128]), [PSK[pr]], [("logit", i)])
                        else:
                            dve(lambda e: e.tensor_tensor(out=logit[:, i * 128:(i + 1) * 128], in0=ps[pr][0:64, 0:128],
                                                          in1=logit[:, i * 128:(i + 1) * 128], op=ALU.add), [PSK[pr], ("logit", i)], [("logit", i)])

                    def p3_iter(i):
                        if i + 1 < NT:
                            p3x(i + 1)
                        if i < NT:
                            p3a(i)
                        if i >= 1:
                            p3b(i - 1)

                    p3x(0)
                    nsb = S // SB
                    for sbk in range(nsb):
                        dft(sbk)
                        if sbk >= 1:
                            p3_iter(2 * (sbk - 1))
                            p3_iter(2 * (sbk - 1) + 1)
                    for i in range(2 * (nsb - 1), NT + 1):
                        p3_iter(i)
                    if mode == "mixer" and b == 0:
                        dma("sp", dbg_yf, yfT[:], r=[("yfT", q_) for q_ in range(8)], w=["dbg_yf"], key="dbg1")
                        dma("sp", dbg_ya, yaT[:], r=["yaT"], w=["dbg_ya"], key="dbg2")
                    P.end_sched()
                    P.end_block()
                s2.close()

        stA.close()
        if mode == "mixer":
            stL.close()
            return nc

        sbw = ExitStack()
        wb = [sbt(sbw, "wb%d" % i, [128, 8, D], BF16) for i in range(NWB)]
        wsrc = (I["w_gate"], I["w_up"], I["w_down"])

        def load_w(m):
            e_, wh = m // 3, m % 3
            if e_ >= NE:
                return
            P.op("pool", lambda e: e.dma_start(out=wb[m % NWB][:], in_=wsrc[wh][e_].rearrange("(k p) c -> p k c", p=128)),
                 (), [("wb", m % NWB)], dma=("wb", m % NWB))

        for m in range(NWB):
            load_w(m)
        with ExitStack() as sr:
            expl = sbt(sr, "expl", [64, S], F32)
            wk = [sbt(sr, "wk%d" % i, [64, S], F32) for i in range(2)]
            vals = sbt(sr, "vals", [64, CAP], F32)
            idxs = sbt(sr, "idxs", [64, CAP], U32)
            idxf = sbt(sr, "idxf", [64, CAP], F32)
            idxTf = sbt(sr, "idxTf", [128, 2, 64], F32)
            act(lambda e: e.activation(out=expl[:], in_=logit[:], func=AF.Exp), [("logit", i) for i in range(NT)], ["expl"])
            for j in range(4):
                mm(ps[j][0:64, :], bd[:], expl[:, j * 512:(j + 1) * 512], True, True, ["bd", "expl"], [PSK[j]])
                dve(lambda e, j=j: e.reciprocal(out=wk[1][:, j * 512:(j + 1) * 512], in_=ps[j][0:64, :]), [PSK[j]], ["wk1"])
            dve(lambda e: e.tensor_tensor(out=wk[0][:], in0=expl[:], in1=wk[1][:], op=ALU.mult), ["expl", "wk1"], ["wk0"])
            cur = 0
            for it in range(CAP // 8):
                c8 = slice(8 * it, 8 * it + 8)
                ck = "wk%d" % cur
                dve(lambda e, cur=cur, c8=c8: e.max(out=vals[:, c8], in_=wk[cur][:]), [ck], [("vals", it)])
                if it < CAP // 8 - 1:
                    dve(lambda e, cur=cur, c8=c8: e.match_replace(out=wk[1 - cur][:], in_to_replace=vals[:, c8], in_values=wk[cur][:],
                                                                  imm_value=-1.0), [ck, ("vals", it)], ["wk%d" % (1 - cur)])
                dve(lambda e, cur=cur, c8=c8: e.max_index(out=idxs[:, c8], in_max=vals[:, c8], in_values=wk[cur][:]), [ck, ("vals", it)], [("idxs", it)])
                if it < CAP // 8 - 1:
                    cur = 1 - cur
            dve(lambda e: e.tensor_copy(out=idxf[:], in_=idxs[:]), [("idxs", q_) for q_ in range(CAP // 8)], ["idxf"])
            for hf in range(2):
                mm(ps[4][:, hf * 64:(hf + 1) * 64], idxf[:, hf * 128:(hf + 1) * 128], identf[0:64, 0:64], True, True,
                   ["idxf", "identf"], [PSK[4]])
                mm(ps[5][:, hf * 64:(hf + 1) * 64], vals[:, hf * 128:(hf + 1) * 128], identf[0:64, 0:64], True, True,
                   [("vals", q_) for q_ in range(CAP // 8)] + ["identf"], [PSK[5]])
            dve(lambda e: e.tensor_scalar(out=idxTf[:], in0=ps[4][:, 0:128].rearrange("p (h c) -> p h c", h=2), scalar1=0.25, scalar2=None, op0=ALU.add), [PSK[4]], ["idxTf"])
            for b in range(1, NB):
                dve(lambda e, b=b: e.tensor_scalar(out=idxTf[:, :, b * 16:(b + 1) * 16], in0=idxTf[:, :, b * 16:(b + 1) * 16],
                                                   scalar1=float(b * S), scalar2=None, op0=ALU.add), ["idxTf"], ["idxTf"])
            dve(lambda e: e.tensor_copy(out=idxU[:], in_=idxTf[:]), ["idxTf"], ["idxU"])
            dve(lambda e: e.tensor_copy(out=valT[:], in_=ps[5][:, 0:128].rearrange("p (h c) -> p h c", h=2)), [PSK[5]], ["valT"])
            for b in range(NB):
                for hh in range(2):
                    pi = 6 + hh
                    mm(ps[pi][:, :], sel4[:, b, :], modrow[:, 1, hh * 512:(hh + 1) * 512], True, True, ["sel4", "modrow"], [PSK[pi]])
                    act(lambda e, b=b, hh=hh, pi=pi: e.activation(out=g2bc[:, b, hh * 512:(hh + 1) * 512], in_=ps[pi][:, :], func=AF.Copy),
                        [PSK[pi]], ["g2bc"])
            P.end_block()

        with ExitStack() as sb_:
            NXG = 8
            xg = [sbt(sb_, "xg%d" % i, [128, D], BF16) for i in range(NXG)]
            xinT2 = [sbt(sb_, "xinT%d" % i, [128, 8, 8 * 128], BF16) for i in range(2)]
            hTe = sbt(sb_, "hTe", [128, 8, 8 * 128], BF16)
            sg = [sbt(sb_, "sg%d" % i, [128, 512], F32) for i in range(2)]
            yv = [sbt(sb_, "yv%d" % i, [128, D], F32) for i in range(NYV)]
            def gathers(e_):
                for sti in range(8):
                    hf, b = sti // 4, sti % 4
                    col = b * 16 + e_
                    P.op("pool", lambda e, sti=sti, hf=hf, b=b, col=col: e.indirect_dma_start(
                        out=xg[sti][:], out_offset=None, in_=h2s_flat,
                        in_offset=bass.IndirectOffsetOnAxis(ap=idxU[:, hf, col:col + 1], axis=0)),
                        ["idxU", ("h2s", b)], [("xg", sti)], dma=("xg", sti))

            def transposes(e_, sti):
                xinT = xinT2[e_ % 2]
                xs = e_ % 2
                b = sti % 4
                for k in range(8):
                    mm(ps[k // 4][:, (k % 4) * 128:(k % 4 + 1) * 128], xg[sti][:, k * 128:(k + 1) * 128], ident[:],
                       True, True, [("xg", sti), "ident"], [PSK[k // 4]])
                for k in range(8):
                    if k < 4:
                        dve(lambda e, k=k: e.tensor_scalar(
                            out=xinT[:, k, sti * 128:(sti + 1) * 128], in0=ps[0][:, (k % 4) * 128:(k % 4 + 1) * 128],
                            scalar1=colG2[:, b, k:k + 1], scalar2=colSH2[:, b, k:k + 1], op0=ALU.mult, op1=ALU.add),
                            [PSK[0], "colG2", "colSH2"], [("xinT", xs, sti, 0)])
                    else:
                        act(lambda e, k=k: e.activation(
                            out=xinT[:, k, sti * 128:(sti + 1) * 128], in_=ps[1][:, (k % 4) * 128:(k % 4 + 1) * 128],
                            func=AF.Identity, bias=colSH2[:, b, k:k + 1], scale=colG2[:, b, k:k + 1]),
                            [PSK[1], "colG2", "colSH2"], [("xinT", xs, sti, 1)])

            gathers(0)
            for sti in range(8):
                transposes(0, sti)
            for e_ in range(NE):
                xinT = xinT2[e_ % 2]
                xs = e_ % 2
                mg, mu, md = 3 * e_, 3 * e_ + 1, 3 * e_ + 2
                wg, wu, wd_ = wb[mg % NWB], wb[mu % NWB], wb[md % NWB]
                if e_ + 1 < NE:
                    gathers(e_ + 1)
                for f in range(8):
                    for sh in range(2):
                        pg, pu = 2 + 2 * ((f * 2 + sh) % 2), 3 + 2 * ((f * 2 + sh) % 2)
                        rd = [("xinT", xs, s_, p_) for s_ in range(sh * 4, sh * 4 + 4) for p_ in range(2)]
                        for k in range(8):
                            mm(ps[pg][:, :], wg[:, k, f * 128:(f + 1) * 128], xinT[:, k, sh * 512:(sh + 1) * 512], k == 0, k == 7,
                               rd + [("wb", mg % NWB)], [PSK[pg]])
                        for k in range(8):
                            mm(ps[pu][:, :], wu[:, k, f * 128:(f + 1) * 128], xinT[:, k, sh * 512:(sh + 1) * 512], k == 0, k == 7,
                               rd + [("wb", mu % NWB)], [PSK[pu]])
                        si = (f * 2 + sh) % 2
                        act(lambda e, si=si, pg=pg: e.activation(out=sg[si][:], in_=ps[pg][:, :], func=AF.Silu), [PSK[pg]], [("sg", si)])
                        dve(lambda e, si=si, pu=pu, f=f, sh=sh: e.tensor_tensor(out=hTe[:, f, sh * 512:(sh + 1) * 512], in0=ps[pu][:, :],
                                                                                in1=sg[si][:], op=ALU.mult),
                            [PSK[pu], ("sg", si)], [("hTe", sh)])
                load_w(mg + NWB)
                load_w(mu + NWB)
                for sti in range(8):
                    hf, b = sti // 4, sti % 4
                    col = b * 16 + e_
                    yi = sti % NYV
                    for dh in range(2):
                        pi = 6 + dh
                        for f in range(8):
                            mm(ps[pi][:, :], hTe[:, f, sti * 128:(sti + 1) * 128], wd_[:, f, dh * 512:(dh + 1) * 512], f == 0, f == 7,
                               [("hTe", sti // 4), ("wb", md % NWB)], [PSK[pi]])
                        dve(lambda e, yi=yi, dh=dh, pi=pi, hf=hf, col=col, b=b: e.scalar_tensor_tensor(
                            out=yv[yi][:, dh * 512:(dh + 1) * 512], in0=ps[pi][:, :], scalar=valT[:, hf, col:col + 1],
                            in1=g2bc[:, b, dh * 512:(dh + 1) * 512], op0=ALU.mult, op1=ALU.mult),
                            [PSK[pi], "valT", "g2bc"], [("yv", yi)])
                    P.op("pool", lambda e, yi=yi, hf=hf, col=col, b=b: e.indirect_dma_start(
                        out=out_flat, out_offset=bass.IndirectOffsetOnAxis(ap=idxU[:, hf, col:col + 1], axis=0),
                        in_=yv[yi][:], in_offset=None, compute_op=ALU.add),
                        ["idxU", ("yv", yi), ("out", b)], [("out", b)], dma=("sc", yi))
                    if e_ + 1 < NE:
                        transposes(e_ + 1, sti)
                load_w(md + NWB)
            P.end_block()
        sbw.close()
        stL.close()
    return nc


_CACHE = {}


def _host_inputs(inp):
    f = np.float32
    g = lambda k: np.asarray(inp[k], dtype=f)
    x = g("x")
    c = g("c")
    w_in = g("w_in")[0]
    b_ada = g("b_ada")[0]
    shared = {
        "rel_bias": np.ascontiguousarray(g("rel_bias")),
        "w_ada": np.ascontiguousarray(g("w_ada")[0]),
        "b_adaT": np.ascontiguousarray(b_ada.reshape(48, 128).T),
        "b_ada_g": np.ascontiguousarray(np.stack([b_ada[2 * D:3 * D], b_ada[5 * D:6 * D]])),
        "gmixT": np.ascontiguousarray(g("norm_mix_g")[0].reshape(8, 128).T),
        "gffnT": np.ascontiguousarray(g("norm_ffn_g")[0].reshape(8, 128).T),
        "w_qkv": np.ascontiguousarray(w_in[:, 512:1280]),
        "w_in_fT": np.ascontiguousarray(w_in[:, 0:512].T),
        "w_fourier": np.ascontiguousarray(g("w_fourier")[0]),
        "bfT": np.ascontiguousarray(g("b_fourier")[0].reshape(4, 128).T),
        "gqk_row": np.ascontiguousarray(np.concatenate([np.tile(g("q_norm_g")[0], 8), np.tile(g("k_norm_g")[0], 2)])),
        "sink": np.ascontiguousarray(g("sink")[0]),
        "w_out": np.ascontiguousarray(g("w_out")[0]),
        "w_router": np.ascontiguousarray(g("w_router")[0]),
        "w_gate": np.ascontiguousarray(g("w_gate")[0]),
        "w_up": np.ascontiguousarray(g("w_up")[0]),
        "w_down": np.ascontiguousarray(g("w_down")[0]),
    }
    shared.update(_consts())
    maps = []
    for r in range(NCORES):
        m = dict(shared)
        m["x"] = np.ascontiguousarray(x[r * NB:(r + 1) * NB])
        m["cT"] = np.ascontiguousarray(c[r * NB:(r + 1) * NB].T)
        maps.append(m)
    return maps


def kernel(**inputs):
    if "nc" not in _CACHE:
        _CACHE["nc"] = build("full")
    nc = _CACHE["nc"]
    maps = _host_inputs(inputs)
    res = run_bass_kernel_spmd(nc, maps, core_ids=list(range(NCORES)))
    outs = [np.asarray(r["out"], dtype=np.float32) for r in res.results]
    return np.concatenate(outs, axis=0)
```
